# Optimizing a Trainium2 kernel written in Bass

```python
import jax, jax.numpy as jnp
from jax import lax
import numpy as np

D_MODEL = 2048
BATCH = 1
SEQ = 8192
DEPTH = 4
DEC_BATCH = 8
DEC_SEQ = 2048
PAST_LEN = 128

N_MIXERS = 2
N_ATTN_LAYERS = (DEPTH + 1) // 2
N_CONV_LAYERS = DEPTH // 2
HEAD_DIM = 128
N_Q_HEADS = D_MODEL // HEAD_DIM
N_KV_HEADS = N_Q_HEADS // 4
GROUP = N_Q_HEADS // N_KV_HEADS
ATTN_WIDTH = N_Q_HEADS * HEAD_DIM
KV_WIDTH = N_KV_HEADS * HEAD_DIM
ATTN_IN = 2 * ATTN_WIDTH + 2 * KV_WIDTH
WINDOW = 128
BLOCK = 128
CONV_WIDTH = D_MODEL
CONV_K = 3
N_META = 16
LEAD = BLOCK
ROPE_THETA = 10000.0
EPS = 1e-6
NEG = -1e30

kernel_name = "hybrid_swa_shortconv_encoder"


def rmsnorm(x, g):
    xf = x.astype(jnp.float32)
    y = xf * lax.rsqrt(jnp.mean(xf * xf, axis=-1, keepdims=True) + EPS)
    return (y * g.astype(jnp.float32)).astype(x.dtype)


def rope(x, pos):
    half = HEAD_DIM // 2
    inv_freq = ROPE_THETA ** (-jnp.arange(0, half, dtype=jnp.float32) * (2.0 / HEAD_DIM))
    ang = pos.astype(jnp.float32)[:, None] * inv_freq[None, :]
    cos = jnp.concatenate([jnp.cos(ang), jnp.cos(ang)], -1)[None, :, None, :]
    sin = jnp.concatenate([jnp.sin(ang), jnp.sin(ang)], -1)[None, :, None, :]
    xf = x.astype(jnp.float32)
    rot = jnp.concatenate([-xf[..., half:], xf[..., :half]], -1)
    return (xf * cos + rot * sin).astype(x.dtype)


def band_blocks(t):
    b, l = t.shape[0], t.shape[1]
    nb = l // BLOCK
    tb = t.reshape((b, nb, BLOCK) + t.shape[2:])
    tp = jnp.pad(tb, [(0, 0), (1, 1)] + [(0, 0)] * (tb.ndim - 2))
    return jnp.concatenate([tp[:, :-2], tp[:, 1:-1], tp[:, 2:]], axis=2)


def attn_branch(h, pos, tok_ok, w_in, w_out, sink):
    B, L, _ = h.shape
    nb = L // BLOCK
    proj = h @ w_in
    q, k, v, z = jnp.split(proj, [ATTN_WIDTH, ATTN_WIDTH + KV_WIDTH, ATTN_WIDTH + 2 * KV_WIDTH], axis=-1)
    q = rope(q.reshape(B, L, N_Q_HEADS, HEAD_DIM), pos)
    k = rope(k.reshape(B, L, N_KV_HEADS, HEAD_DIM), pos)
    v = v.reshape(B, L, N_KV_HEADS, HEAD_DIM)
    qb = q.reshape(B, nb, BLOCK, N_KV_HEADS, GROUP, HEAD_DIM)
    kb, vb = band_blocks(k), band_blocks(v)
    k_meta = k[:, LEAD - N_META:LEAD]
    v_meta = v[:, LEAD - N_META:LEAD]
    pq = pos.reshape(nb, BLOCK)
    pk = band_blocks(pos[None])[0]
    okk = band_blocks(tok_ok[None])[0]
    band_mask = okk[:, None, :] & (jnp.abs(pq[:, :, None] - pk[:, None, :]) <= WINDOW)
    p_meta = pos[LEAD - N_META:LEAD]
    meta_mask = jnp.abs(pq[:, :, None] - p_meta[None, None, :]) > WINDOW
    mask = jnp.concatenate([band_mask, meta_mask], axis=-1)[None, :, None, None]
    scale = HEAD_DIM ** -0.5
    s_band = jnp.einsum('bnqhgd,bnkhd->bnhgqk', qb, kb)
    s_meta = jnp.einsum('bnqhgd,bkhd->bnhgqk', qb, k_meta)
    s = jnp.concatenate([s_band, s_meta], axis=-1).astype(jnp.float32) * scale
    s = jnp.where(mask, s, NEG)
    sink_l = sink.astype(jnp.float32).reshape(N_KV_HEADS, GROUP)[None, None, :, :, None, None]
    m = jnp.maximum(jnp.max(s, axis=-1, keepdims=True), sink_l)
    e = jnp.exp(s - m)
    p = (e / (jnp.sum(e, axis=-1, keepdims=True) + jnp.exp(sink_l - m))).astype(v.dtype)
    o = (jnp.einsum('bnhgqk,bnkhd->bnqhgd', p[..., :3 * BLOCK], vb)
         + jnp.einsum('bnhgqk,bkhd->bnqhgd', p[..., 3 * BLOCK:], v_meta))
    o = o.reshape(B, L, ATTN_WIDTH)
    return (o * jax.nn.silu(z)) @ w_out


def conv_branch(h, tok_ok, w_in, conv_w, w_out):
    proj = h @ w_in
    b_gate, c_gate, u, z = jnp.split(proj, 4, axis=-1)
    u = jnp.where(tok_ok[None, :, None], c_gate * u, 0.0)
    up = jnp.pad(u, ((0, 0), (1, 1), (0, 0)))
    y = up[:, :-2] * conv_w[0] + up[:, 1:-1] * conv_w[1] + up[:, 2:] * conv_w[2]
    return ((b_gate * y) * jax.nn.silu(z)) @ w_out


def trunk(x, meta_tokens, norm_w, attn_w_in, attn_w_out, attn_sink, conv_w_in, conv_w, conv_w_out, final_norm_w):
    B, S, D = x.shape
    lead = jnp.concatenate([jnp.zeros((LEAD - N_META, D), x.dtype), meta_tokens.astype(x.dtype)], axis=0)
    h = jnp.concatenate([jnp.broadcast_to(lead[None], (B, LEAD, D)), x], axis=1)
    L = LEAD + S
    pos = jnp.arange(L, dtype=jnp.int32) - (LEAD - N_META)
    tok_ok = pos >= 0
    for i in range(DEPTH):
        hn = rmsnorm(h, norm_w[i])
        j = i // N_MIXERS
        if i % N_MIXERS == 0:
            h = h + attn_branch(hn, pos, tok_ok, attn_w_in[j], attn_w_out[j], attn_sink[j])
        else:
            h = h + conv_branch(hn, tok_ok, conv_w_in[j], conv_w[j], conv_w_out[j])
    return rmsnorm(h, final_norm_w)[:, LEAD:]


def setup_inputs(seed: int = 0) -> dict:
    key = jax.random.key(seed)
    ks = jax.random.split(key, 12)
    f32 = jnp.float32
    return {
        "x_prompt": jax.random.normal(ks[0], (BATCH, SEQ, D_MODEL), f32),
        "x_sample": jax.random.normal(ks[1], (DEC_BATCH, DEC_SEQ, D_MODEL), f32),
        "meta_tokens": jax.random.normal(ks[2], (N_META, D_MODEL), f32),
        "norm_w": 1.0 + 0.02 * jax.random.normal(ks[3], (DEPTH, D_MODEL), f32),
        "attn_w_in": jax.random.normal(ks[4], (N_ATTN_LAYERS, D_MODEL, ATTN_IN), f32) * D_MODEL ** -0.5,
        "attn_w_out": jax.random.normal(ks[5], (N_ATTN_LAYERS, ATTN_WIDTH, D_MODEL), f32) * ATTN_WIDTH ** -0.5,
        "attn_sink": 0.5 * jax.random.normal(ks[6], (N_ATTN_LAYERS, N_Q_HEADS), f32),
        "conv_w_in": jax.random.normal(ks[7], (N_CONV_LAYERS, D_MODEL, 4 * CONV_WIDTH), f32) * D_MODEL ** -0.5,
        "conv_w": jax.random.normal(ks[8], (N_CONV_LAYERS, CONV_K, CONV_WIDTH), f32) * CONV_K ** -0.5,
        "conv_w_out": jax.random.normal(ks[9], (N_CONV_LAYERS, CONV_WIDTH, D_MODEL), f32) * CONV_WIDTH ** -0.5,
        "final_norm_w": 1.0 + 0.02 * jax.random.normal(ks[10], (D_MODEL,), f32),
    }


def reference(x_prompt, x_sample, meta_tokens, norm_w, attn_w_in, attn_w_out, attn_sink, conv_w_in, conv_w, conv_w_out, final_norm_w):
    y_prompt = trunk(x_prompt, meta_tokens, norm_w, attn_w_in, attn_w_out, attn_sink, conv_w_in, conv_w, conv_w_out, final_norm_w)
    y_sample = trunk(x_sample, meta_tokens, norm_w, attn_w_in, attn_w_out, attn_sink, conv_w_in, conv_w, conv_w_out, final_norm_w)
    return (y_prompt, y_sample)
```

```python
import numpy as np
import ml_dtypes
import concourse.bass as bass
import concourse.mybir as mybir
from concourse.bass_utils import run_bass_kernel_spmd

F32 = mybir.dt.float32
BF16 = mybir.dt.bfloat16
AF = mybir.ActivationFunctionType
ALU = mybir.AluOpType
AX = mybir.AxisListType

D = 2048
KC = 16
NB = 17
L = NB * 128
NCORES = 8
HEAD = 128
SCALE = HEAD ** -0.5
EPS = 1e-6
NEGB = -30000.0
SEG_TILES = {
    0: [[(0, 9), (9, 17)], [(0, 9), (9, 17)], [(0, 9), (9, 17)], [(1, 9), (9, 17)]],
    1: [[(0, 9), (9, 17)], [(0, 9), (9, 16)], [(5, 15)], [(6, 14)]],
}
TMAX = max(b1 - b0 for sg in SEG_TILES.values() for tl in sg for b0, b1 in tl) * 128
RMAX = TMAX + 256
NSLOT = 4
USE_LO = False
OUT_BLOCKS = {0: list(range(1, 17)), 1: list(range(6, 14))}
N_LAYERS = 4

ENGINES = ("tensor", "vector", "scalar", "gpsimd", "sync")


class Prog:
    def __init__(self, nc):
        self.nc = nc
        self.ops = {e: [] for e in ENGINES}
        self.cnt = {e: 0 for e in ENGINES}
        self.seen = {e: {} for e in ENGINES}
        self.last_w = {}
        self.readers = {}
        self.dma_sems = {}
        self.sem_handles = {}
        self._ctx = []
        for e in ENGINES:
            cm = nc.semaphore("s_" + e)
            h = cm.__enter__()
            self._ctx.append(cm)
            self.sem_handles[("eng", e)] = h

    def dma_sem(self, key):
        if key not in self.dma_sems:
            cm = self.nc.semaphore("d_" + "".join(ch if ch.isalnum() else "_" for ch in str(key)))
            h = cm.__enter__()
            self._ctx.append(cm)
            self.dma_sems[key] = [h, 0]
            self.sem_handles[("dma", key)] = h
        return self.dma_sems[key]

    def _deps(self, reads, writes):
        ev = []
        for r in reads:
            w = self.last_w.get(r)
            if w is not None:
                ev.append(w)
        for w_ in writes:
            w = self.last_w.get(w_)
            if w is not None:
                ev.append(w)
            ev.extend(self.readers.get(w_, ()))
        return ev

    def _waits(self, eng, events):
        need = {}
        seen = self.seen[eng]
        for (k, v) in events:
            if seen.get(k, 0) >= v:
                continue
            if need.get(k, 0) < v:
                need[k] = v
        for k, v in need.items():
            seen[k] = v
        return list(need.items())

    def _record(self, reads, writes, event):
        for r in reads:
            self.readers.setdefault(r, []).append(event)
        for w in writes:
            self.last_w[w] = event
            self.readers[w] = []

    def op(self, eng, fn, reads=(), writes=()):
        waits = self._waits(eng, self._deps(reads, writes))
        self.cnt[eng] += 1
        event = (("eng", eng), self.cnt[eng])
        self.ops[eng].append((waits, fn, [(("eng", eng), 1)]))
        self._record(reads, writes, event)
        return event

    def mm(self, fns, reads=(), writes=()):
        eng = "tensor"
        waits = self._waits(eng, self._deps(reads, writes))
        self.cnt[eng] += 1
        event = (("eng", eng), self.cnt[eng])
        n = len(fns)
        for i, fn in enumerate(fns):
            self.ops[eng].append((waits if i == 0 else [], fn, [(("eng", eng), 1)] if i == n - 1 else []))
        self._record(reads, writes, event)
        return event

    def dma(self, eng, fn, semkey, reads=(), writes=()):
        waits = self._waits(eng, self._deps(reads, writes))
        s = self.dma_sem(semkey)
        s[1] += 16
        event = (("dma", semkey), s[1])
        self.ops[eng].append((waits, fn, [(("dma", semkey), 16)]))
        self._record(reads, writes, event)
        return event

    def barrier(self):
        evs = self.all_events()
        for e in ENGINES:
            self.wait_all(e, evs)

    def wait_all(self, eng, events):
        waits = self._waits(eng, list(events))
        if waits:
            self.ops[eng].append((waits, None, []))

    def all_events(self):
        evs = [(("eng", e), self.cnt[e]) for e in ENGINES if self.cnt[e] > 0]
        evs += [(("dma", k), v[1]) for k, v in self.dma_sems.items() if v[1] > 0]
        return evs

    def emit(self):
        nc = self.nc
        handles = self.sem_handles
        with nc.Block() as block:
            for e in ENGINES:
                ops = self.ops[e]
                if not ops:
                    continue

                def body(engobj, ops=ops):
                    for waits, fn, incs in ops:
                        for k, v in waits:
                            engobj.wait_ge(handles[k], v)
                        if fn is None:
                            continue
                        ins = fn(engobj)
                        for k, n in incs:
                            ins.then_inc(handles[k], n)

                getattr(block, e)(body)

    def close(self):
        for cm in reversed(self._ctx):
            cm.__exit__(None, None, None)


def subtiles(s, e, n=512):
    out = []
    while s < e:
        out.append((s, min(s + n, e)))
        s += n
    return out


def weight_schedule():
    sched = []
    for seg in range(2):
        for l in range(N_LAYERS):
            j = l // 2
            for _tile in SEG_TILES[seg][l]:
                if l % 2 == 0:
                    for g in range(4):
                        for c in (4, 5, 0, 1, 2, 3, 6, 7, 8, 9):
                            sched.append(("w_ai", (j * 4 + g) * 10 + c))
                    for m in range(KC):
                        sched.append(("w_ao", j * KC + m))
                else:
                    for f in range(KC):
                        for c in (1, 2, 0, 3):
                            sched.append(("w_ci", (j * KC + f) * 4 + c))
                    for m in range(KC):
                        sched.append(("w_co", j * KC + m))
    return sched


def build_nc():
    nc = bass.Bass("TRN2", target_bir_lowering=False)
    dt_in = lambda n, s: nc.dram_tensor(n, s, F32, kind="ExternalInput").ap()
    xs = dt_in("xs", [2, NB, 128, D])
    cosd = dt_in("cosd", [2, 128, L])
    sind = dt_in("sind", [2, 128, L])
    okd = dt_in("okd", [2, L])
    kbd = dt_in("kbd", [2, 128, 3 * NB])
    wd = {
        "w_ai": dt_in("w_ai", [2 * 4 * 10, 128, D]),
        "w_ao": dt_in("w_ao", [2 * KC, 128, D]),
        "w_ci": dt_in("w_ci", [2 * KC * 4, 128, D]),
        "w_co": dt_in("w_co", [2 * KC, 128, D]),
    }
    gvd = dt_in("gvd", [128, 64])
    gfd = dt_in("gfd", [128, KC])
    skd = dt_in("skd", [32])
    cwd = dt_in("cwd", [128, 96])
    idd = dt_in("idd", [128, 128])
    trid = dt_in("trid", [128, 256])
    ysd = nc.dram_tensor("ys", [16, 128, D], F32, kind="ExternalOutput").ap()
    ypd = nc.dram_tensor("yp", [8, 128, D], F32, kind="ExternalOutput").ap()
    outd = {0: ysd, 1: ypd}

    P = Prog(nc)
    ctx = []

    def sb(name, shape, dt):
        cm = nc.sbuf_tensor(name, shape, dt)
        t = cm.__enter__()
        ctx.append(cm)
        return t

    hi = sb("hi", [128, KC, L], BF16)
    lo = sb("lo", [128, KC, L], BF16) if USE_LO else None
    scrA = sb("scrA", [128, KC * TMAX], BF16)
    og = scrA[:, :].rearrange("p (a b) -> p a b", a=KC)
    NXIN = 5
    xin = scrA[:, 0:NXIN * 4096].bitcast(F32).rearrange("p (a b) -> p a b", a=NXIN)
    hf = scrA[:, 8192:12288].bitcast(F32).rearrange("p (a b) -> p a b", a=KC)
    assert KC * TMAX >= NXIN * 4096
    assert KC * TMAX >= 16384
    o_sz, o_kT, o_vv = 0, 4 * TMAX, 4 * TMAX + 2 * RMAX
    o_ck = o_vv + 2 * RMAX
    o_cv, o_km, o_vm = o_ck + 512, o_ck + 1024, o_ck + 1024 + 64
    o_pt = o_vm + 512
    o_sin = o_pt + 4096
    nB = o_sin + 2 * RMAX
    assert nB >= 2 * (2 * (TMAX + 2)) + 2 * 2 * TMAX + 2 * TMAX
    scrB = sb("scrB", [128, nB], BF16)
    sz = scrB[:, o_sz:o_sz + 4 * TMAX].rearrange("p (a b) -> p a b", a=4)
    kT = scrB[:, o_kT:o_kT + 2 * RMAX].rearrange("p (a b) -> p a b", a=2)
    vv = scrB[:, o_vv:o_vv + 2 * RMAX].rearrange("p (a c b) -> p a c b", a=2, b=128)
    ck = scrB[:, o_ck:o_ck + 512].rearrange("p (a b) -> p a b", a=4)
    cv = scrB[:, o_cv:o_cv + 512].rearrange("p (a b) -> p a b", a=4)
    kmeta = scrB[:, o_km:o_km + 64].rearrange("p (a b) -> p a b", a=4)
    vmeta = scrB[:, o_vm:o_vm + 512].rearrange("p (a b) -> p a b", a=4)
    PT = scrB[:, o_pt:o_pt + 4096].rearrange("p (a b c) -> p a b c", a=2, b=4)
    sinr = scrB[:, o_sin:o_sin + 2 * RMAX].bitcast(F32)
    c0 = 2 * 2 * (TMAX + 2)
    cu = scrB[:, 0:c0].bitcast(F32).rearrange("p (a b) -> p a b", a=2)
    gate = scrB[:, c0:c0 + 4 * TMAX].bitcast(F32).rearrange("p (a b) -> p a b", a=2)
    ybuf = scrB[:, c0 + 4 * TMAX:c0 + 6 * TMAX].bitcast(F32)
    wsl = sb("wsl", [128, NSLOT, D], BF16)
    rstd = sb("rstd", [128, L], F32)
    rcol = sb("rcol", [128, RMAX // 128 + 1], F32)
    cosr = sb("cosr", [128, RMAX], F32)
    NTMP = 8
    tmp = sb("tmp", [128, NTMP, 512], F32)
    NSQ = 5
    sq = sb("sq", [128, NSQ, 512], BF16)
    ccu = sb("ccu", [128, KC], F32)
    csm = sb("csm", [128, 2, 16], F32)
    identf = sb("identf", [128, 128], F32)
    identb = sb("identb", [128, 128], BF16)
    onesb = sb("onesb", [128, 128], BF16)
    onesf = sb("onesf", [128, 4], F32)
    epst = sb("epst", [128, 1], F32)
    trib = sb("trib", [128, 2, 128], BF16)
    gv = sb("gv", [128, 64], F32)
    ginv = sb("ginv", [128, 80], F32)
    gft = sb("gft", [128, KC], F32)
    gN = sb("gN", [128, 64], F32)
    ratio = sb("ratio", [128, 64], F32)
    cw = sb("cw", [128, 96], F32)
    esink = sb("esink", [128, 32], F32)
    mx = sb("mx", [128, 2, 20], F32)
    negM = sb("negM", [128, 2], F32)
    kbM = sb("kbM", [128, 2, 3 * NB], F32)
    esk = sb("esk", [128, 2, 4], F32)
    eskrow = sb("eskrow", [128, 2, 512], BF16)
    kb = sb("kb", [128, 3 * NB], F32)
    ss = sb("ss", [128, 8], F32)
    rso = sb("rso", [128, 2], F32)

    psum = []
    for i in range(8):
        cm = nc.psum_tensor("ps%d" % i, [128, 512], F32)
        psum.append(cm.__enter__())
        ctx.append(cm)

    rot = {"gen": 0, "tmp": 0, "sq": 0, "pt": 0, "x": 0}

    def gbank(pool):
        i = pool[rot["gen"] % len(pool)]
        rot["gen"] += 1
        return i

    def gtmp():
        i = rot["tmp"] % NTMP
        rot["tmp"] += 1
        return i

    def hkeys(s, e):
        return [("hi", kc, b) for kc in range(KC) for b in range(s // 128, (e + 127) // 128)]

    wsched = weight_schedule()
    wstate = {"issued": 0, "next": 0}

    def w_issue(upto):
        while wstate["issued"] < min(upto, len(wsched)):
            i = wstate["issued"]
            name, idx = wsched[i]
            slot = i % NSLOT
            P.dma("gpsimd", (lambda e, name=name, idx=idx, slot=slot: e.dma_start(out=wsl[:, slot, :], in_=wd[name][idx])),
                  ("w", slot), writes=[("w", slot)])
            wstate["issued"] += 1

    def w_acquire(name, idx):
        i = wstate["next"]
        assert wsched[i] == (name, idx), (i, wsched[i], name, idx)
        w_issue(i + NSLOT - 1)
        wstate["next"] += 1
        return i % NSLOT

    def w_release():
        w_issue(wstate["next"] + NSLOT - 1)

    ld = lambda out_, in_, key, wkey: P.dma("sync", (lambda e: e.dma_start(out=out_, in_=in_)), key, writes=[wkey])
    ld(gv[:], gvd[:, :], "c_gv", "gv")
    ld(cw[:], cwd[:, :], "c_cw", "cw")
    ld(identf[:], idd[:, :], "c_id", "identf")
    ld(esink[:], skd.partition_broadcast(128), "c_sk", "esink")
    P.dma("gpsimd", lambda e: e.dma_start(out=trib[:], in_=trid.rearrange("p (a b) -> p a b", a=2)), "c2", writes=["trib"])
    P.op("vector", lambda e: e.tensor_copy(out=identb[:], in_=identf[:]), reads=["identf"], writes=["identb"])
    P.op("vector", lambda e: e.memset(onesb[:], 1.0), writes=["onesb"])
    P.op("vector", lambda e: e.memset(onesf[:], 1.0), writes=["onesf"])
    P.op("vector", lambda e: e.memset(epst[:], EPS), writes=["epst"])
    P.op("vector", lambda e: e.reciprocal(out=ginv[:, 0:64], in_=gv[:]), reads=["gv"], writes=["ginv"])
    P.op("vector", lambda e: e.memset(ginv[:, 64:80], 1.0), reads=["ginv"], writes=["ginv"])
    ld(gft[:], gfd[:, :], "c_gf", "gft")
    P.op("vector", lambda e: e.tensor_copy(out=gN[:, 0:48], in_=gv[:, 16:64]), reads=["gv"], writes=["gN"])
    P.op("vector", lambda e: e.memset(gN[:, 48:64], 1.0), reads=["gN"], writes=["gN"])
    P.op("vector", lambda e: e.tensor_tensor(out=ratio[:], in0=gN[:], in1=ginv[:, 0:64], op=ALU.mult), reads=["gN", "ginv"], writes=["ratio"])

    ALLB = list(range(8))

    def update(l, m, s, e, bank, n):
        lm = l * KC + m
        t1 = gtmp()
        P.op("scalar", lambda en: en.activation(out=tmp[:, t1, 0:n], in_=psum[bank][:, 0:n], func=AF.Copy, scale=gN[:, lm:lm + 1]),
             reads=[("ps", bank), "gN"], writes=[("tmp", t1)])
        hk = [("hi", m, b) for b in range(s // 128, (e + 127) // 128)]
        if not USE_LO:
            P.op("vector", lambda en: en.scalar_tensor_tensor(out=hi[:, m, s:e], in0=hi[:, m, s:e], scalar=ratio[:, lm:lm + 1],
                                                              in1=tmp[:, t1, 0:n], op0=ALU.mult, op1=ALU.add),
                 reads=hk + [("tmp", t1), "ratio"], writes=hk)
        else:
            lk = [("lo", m, b) for b in range(s // 128, (e + 127) // 128)]
            t2 = gtmp()
            P.op("vector", lambda en: en.scalar_tensor_tensor(out=tmp[:, t2, 0:n], in0=hi[:, m, s:e], scalar=ratio[:, lm:lm + 1],
                                                              in1=tmp[:, t1, 0:n], op0=ALU.mult, op1=ALU.add),
                 reads=hk + [("tmp", t1), "ratio"], writes=[("tmp", t2)])
            P.op("vector", lambda en: en.scalar_tensor_tensor(out=tmp[:, t1, 0:n], in0=lo[:, m, s:e], scalar=ratio[:, lm:lm + 1],
                                                              in1=tmp[:, t2, 0:n], op0=ALU.mult, op1=ALU.add),
                 reads=lk + [("tmp", t2), "ratio"], writes=[("tmp", t1)])
            P.op("scalar", lambda en: en.activation(out=hi[:, m, s:e], in_=tmp[:, t1, 0:n], func=AF.Copy),
                 reads=[("tmp", t1)], writes=hk)
            P.op("vector", lambda en: en.tensor_tensor(out=lo[:, m, s:e], in0=tmp[:, t1, 0:n], in1=hi[:, m, s:e], op=ALU.subtract),
                 reads=hk + [("tmp", t1)], writes=lk)

    def rsk(s, e):
        return [("rs", b_) for b_ in range(s // 128, (e + 127) // 128)]

    def out_proj(l, wname, j, t0, t1):
        subs = subtiles(t0, t1)
        fuse = True
        rb = [5, 6, 7][:len(subs)] if fuse else []
        pool = [0, 1, 2, 3, 4] if fuse else ALLB
        pend = []

        def flush_one():
            m_, qi_, n_, si_ = pend.pop(0)
            P.mm([lambda en: en.matmul(psum[rb[si_]][:, 0:n_], onesb[:], sq[:, qi_, 0:n_], start=(m_ == 0), stop=(m_ == KC - 1))],
                 reads=[("sq", qi_), "onesb"], writes=[("ps", rb[si_])])

        for m in range(KC):
            slot = w_acquire(wname, j * KC + m)
            for si, (s, e) in enumerate(subs):
                n = e - s
                bank = gbank(pool)
                fns = []
                for kc in range(KC):
                    fns.append(lambda en, kc=kc, s=s, e=e, n=n, bank=bank, slot=slot: en.matmul(
                        psum[bank][:, 0:n], wsl[:, slot, kc * 128:(kc + 1) * 128], og[:, kc, s - t0:e - t0],
                        start=(kc == 0), stop=(kc == KC - 1)))
                ogk = [("og", kc, b_) for kc in range(KC) for b_ in range((s - t0) // 128, (e - t0 + 127) // 128)]
                P.mm(fns, reads=[("w", slot)] + ogk, writes=[("ps", bank)])
                update(l, m, s, e, bank, n)
                if fuse:
                    qi = rot["sq"] % NSQ
                    rot["sq"] += 1
                    lk = (l + 1) * KC + m
                    P.op("scalar", lambda en, m=m, qi=qi, lk=lk, n=n, s=s, e=e: en.activation(out=sq[:, qi, 0:n], in_=hi[:, m, s:e], func=AF.Square,
                                                                                scale=ginv[:, lk:lk + 1]),
                         reads=[("hi", m, b_) for b_ in range(s // 128, e // 128)] + ["ginv"], writes=[("sq", qi)])
                    pend.append((m, qi, n, si))
                    if len(pend) > 2:
                        flush_one()
            w_release()
        while pend:
            flush_one()
        if fuse:
            for si, (s, e) in enumerate(subs):
                n = e - s
                t = gtmp()
                P.op("scalar", lambda en, t=t, si=si, n=n: en.activation(out=tmp[:, t, 0:n], in_=psum[rb[si]][:, 0:n], func=AF.Sqrt,
                                                                         bias=epst[:, 0:1], scale=1.0 / D),
                     reads=[("ps", rb[si]), "epst"], writes=[("tmp", t)])
                P.op("vector", lambda en, t=t, n=n, s=s, e=e: en.reciprocal(out=rstd[:, s:e], in_=tmp[:, t, 0:n]),
                     reads=[("tmp", t)], writes=rsk(s, e))

    def rstd_phase(l, r0, r1):
        for (s, e) in subtiles(r0, r1):
            n = e - s
            bank = gbank(ALLB)
            for kc in range(KC):
                qi = rot["sq"] % NSQ
                rot["sq"] += 1
                lk = l * KC + kc
                P.op("scalar", lambda en, kc=kc, qi=qi, lk=lk, n=n, s=s, e=e: en.activation(out=sq[:, qi, 0:n], in_=hi[:, kc, s:e], func=AF.Square,
                                                                             scale=ginv[:, lk:lk + 1]),
                     reads=[("hi", kc, b_) for b_ in range(s // 128, (e + 127) // 128)] + ["ginv"], writes=[("sq", qi)])
                P.mm([lambda en, kc=kc, qi=qi, bank=bank, n=n: en.matmul(psum[bank][:, 0:n], onesb[:], sq[:, qi, 0:n],
                                                                    start=(kc == 0), stop=(kc == KC - 1))],
                     reads=[("sq", qi), "onesb"], writes=[("ps", bank)])
            t = gtmp()
            P.op("scalar", lambda en, t=t, bank=bank, n=n: en.activation(out=tmp[:, t, 0:n], in_=psum[bank][:, 0:n], func=AF.Sqrt,
                                                                    bias=epst[:, 0:1], scale=1.0 / D),
                 reads=[("ps", bank), "epst"], writes=[("tmp", t)])
            P.op("vector", lambda en, t=t, n=n, s=s, e=e: en.reciprocal(out=rstd[:, s:e], in_=tmp[:, t, 0:n]),
                 reads=[("tmp", t)], writes=rsk(s, e))

    def rope_evac(bank, n, cfn, sfn, tkeys, dst_fn, dst_keys):
        ta, tb = gtmp(), gtmp()
        P.op("vector", lambda en: en.tensor_tensor(out=tmp[:, ta, 0:n], in0=psum[bank][:, 0:n], in1=cfn(0, 128), op=ALU.mult),
             reads=[("ps", bank)] + tkeys, writes=[("tmp", ta)])
        P.op("vector", lambda en: en.tensor_tensor(out=tmp[0:64, tb, 0:n], in0=psum[bank][64:128, 0:n], in1=sfn(64, 128), op=ALU.mult),
             reads=[("ps", bank)] + tkeys, writes=[("tmp", tb)])
        P.op("vector", lambda en: en.tensor_tensor(out=tmp[64:128, tb, 0:n], in0=psum[bank][0:64, 0:n], in1=sfn(0, 64), op=ALU.mult),
             reads=[("ps", bank)] + tkeys, writes=[("tmp", tb)])
        P.op("vector", lambda en: en.tensor_tensor(out=dst_fn(), in0=tmp[:, ta, 0:n], in1=tmp[:, tb, 0:n], op=ALU.add),
             reads=[("tmp", ta), ("tmp", tb)], writes=dst_keys)

    def proj_mm(slot, s, e, bank):
        n = e - s
        fns = [lambda en, kc=kc: en.matmul(psum[bank][:, 0:n], wsl[:, slot, kc * 128:(kc + 1) * 128], hi[:, kc, s:e],
                                           start=(kc == 0), stop=(kc == KC - 1)) for kc in range(KC)]
        P.mm(fns, reads=[("w", slot)] + hkeys(s, e), writes=[("ps", bank)])

    def attn_tile(seg, l, b0, b1, has_carry, has_next, first):
        j = l // 2
        t0, t1 = b0 * 128, b1 * 128
        kv_lo = b0 if (has_carry or b0 == 0) else b0 - 1
        r1b = min(b1 + 1, NB)
        r0, r1 = kv_lo * 128, r1b * 128
        GB = [0, 1, 5]
        meta_in = (kv_lo == 0)
        nr = r1 - r0
        RS = rsk(r0, r1)
        P.dma("sync", lambda e: e.dma_start(out=cosr[:, 0:nr], in_=cosd[seg][:, r0:r1]), "cos", writes=["cosr"])
        P.dma("sync", lambda e: e.dma_start(out=sinr[:, 0:nr], in_=sind[seg][:, r0:r1]), "sin", writes=["sinr"])
        P.op("vector", lambda e: e.tensor_tensor(out=cosr[:, 0:nr], in0=cosr[:, 0:nr], in1=rstd[:, r0:r1], op=ALU.mult),
             reads=["cosr"] + RS, writes=["cosr"])
        P.op("vector", lambda e: e.tensor_tensor(out=sinr[:, 0:nr], in0=sinr[:, 0:nr], in1=rstd[:, r0:r1], op=ALU.mult),
             reads=["sinr"] + RS, writes=["sinr"])
        if first and not meta_in:
            P.dma("sync", lambda e: e.dma_start(out=csm[:, 0, :], in_=cosd[seg][:, 112:128]), "csm", writes=["csm"])
            P.dma("sync", lambda e: e.dma_start(out=csm[:, 1, :], in_=sind[seg][:, 112:128]), "csm", writes=["csm"])
            P.op("vector", lambda e: e.tensor_tensor(out=csm[:, :, :], in0=csm[:, :, :], in1=rstd[:, 112:128].unsqueeze(1).broadcast_to([128, 2, 16]), op=ALU.mult),
                 reads=["csm"] + rsk(0, 128), writes=["csm"])
        nblk = r1b - kv_lo
        bank = gbank(GB)
        fns = [lambda en, bi=bi, bank=bank: en.matmul(psum[bank][:, bi:bi + 1], rstd[0:1, r0 + bi * 128:r0 + (bi + 1) * 128], onesf[0:1, 0:1],
                                                      start=True, stop=True) for bi in range(nblk)]
        rr = RS + ["onesf"]
        if first:
            fns.append(lambda en, bank=bank: en.matmul(psum[bank][0:16, nblk:nblk + 1], rstd[0:1, 112:128], onesf[0:1, 0:1], start=True, stop=True))
            rr = rr + rsk(0, 128)
        P.mm(fns, reads=rr, writes=[("ps", bank)])
        ncol = nblk + 1
        P.op("vector", lambda e, bank=bank: e.tensor_copy(out=rcol[:, 0:ncol], in_=psum[bank][:, 0:ncol]),
             reads=[("ps", bank)], writes=["rcol"])
        cfn = lambda c0, n: (lambda p0, p1: cosr[p0:p1, c0:c0 + n])
        sfn = lambda c0, n: (lambda p0, p1: sinr[p0:p1, c0:c0 + n])

        def kvq(g):
            base = (j * 4 + g) * 10
            gb = g % 2
            slot = w_acquire("w_ai", base + 4)
            for (s, e) in subtiles(r0, r1):
                n = e - s
                bank = gbank(GB)
                proj_mm(slot, s, e, bank)
                rope_evac(bank, n, cfn(s - r0, n), sfn(s - r0, n), ["cosr", "sinr"], (lambda s=s, e=e: kT[:, gb, s - r0:e - r0]),
                          [("kT", gb, b_) for b_ in range((s - r0) // 128, (e - r0) // 128)])
                yield
            if first and meta_in:
                P.op("scalar", lambda en: en.activation(out=kmeta[:, g, :], in_=kT[:, gb, 112:128], func=AF.Copy),
                     reads=[("kT", gb, 0)], writes=[("kmeta", g)])
            elif first:
                bank = gbank(GB)
                proj_mm(slot, 112, 128, bank)
                rope_evac(bank, 16, (lambda p0, p1: csm[p0:p1, 0, :]), (lambda p0, p1: csm[p0:p1, 1, :]), ["csm"],
                          (lambda: kmeta[:, g, :]), [("kmeta", g)])
                yield
            w_release()
            slot = w_acquire("w_ai", base + 5)
            for bi in range(nblk):
                s = r0 + bi * 128
                if bi % 4 == 0:
                    bank = gbank(GB)
                c0 = (bi % 4) * 128
                fns = [lambda en, kc=kc, s=s, bank=bank, slot=slot, c0=c0: en.matmul(
                    psum[bank][:, c0:c0 + 128], hi[:, kc, s:s + 128], wsl[:, slot, kc * 128:(kc + 1) * 128], start=(kc == 0), stop=(kc == KC - 1))
                    for kc in range(KC)]
                P.mm(fns, reads=[("w", slot)] + hkeys(s, s + 128), writes=[("ps", bank)])
                P.op("scalar", lambda en, bi=bi, bank=bank, c0=c0: en.activation(out=vv[:, gb, bi, :], in_=psum[bank][:, c0:c0 + 128], func=AF.Copy,
                                                                                 scale=rcol[:, bi:bi + 1]),
                     reads=[("ps", bank), "rcol"], writes=[("v", gb, bi)])
                yield
            if first:
                bank = gbank(GB)
                fns = [lambda en, kc=kc, bank=bank, slot=slot: en.matmul(
                    psum[bank][0:16, 0:128], hi[:, kc, 112:128], wsl[:, slot, kc * 128:(kc + 1) * 128], start=(kc == 0), stop=(kc == KC - 1))
                    for kc in range(KC)]
                P.mm(fns, reads=[("w", slot)] + hkeys(0, 128), writes=[("ps", bank)])
                P.op("scalar", lambda en, bank=bank: en.activation(out=vmeta[0:16, g, :], in_=psum[bank][0:16, 0:128], func=AF.Copy,
                                                                   scale=rcol[0:16, nblk:nblk + 1]),
                     reads=[("ps", bank), "rcol"], writes=[("vmeta", g)])
                yield
            w_release()
            for h in range(4):
                slot = w_acquire("w_ai", base + h)
                for (s, e) in subtiles(t0, t1):
                    n = e - s
                    bank = gbank(GB)
                    proj_mm(slot, s, e, bank)
                    rope_evac(bank, n, cfn(s - r0, n), sfn(s - r0, n), ["cosr", "sinr"],
                              (lambda s=s, e=e, h=h: og[:, 4 * g + h, s - t0:e - t0]),
                              [("og", 4 * g + h, b_) for b_ in range((s - t0) // 128, (e - t0) // 128)])
                    yield
                w_release()

        def stab(g):
            gb = g % 2
            srcs = []
            for h in range(4):
                for (s_, e_) in subtiles(0, t1 - t0):
                    srcs.append((lambda h=h, s_=s_, e_=e_: og[:, 4 * g + h, s_:e_], e_ - s_,
                                 [("og", 4 * g + h, b_) for b_ in range(s_ // 128, e_ // 128)], True))
            for (s_, e_) in subtiles(0, nr):
                srcs.append((lambda s_=s_, e_=e_: kT[:, gb, s_:e_], e_ - s_, [("kT", gb, b_) for b_ in range(s_ // 128, e_ // 128)], False))
            if has_carry:
                srcs.append((lambda: ck[:, g, :], 128, [("ck", g)], False))
            srcs.append((lambda: kmeta[:, g, :], 16, [("kmeta", g)], False))
            nq = sum(1 for x in srcs if x[3])
            nk = len(srcs) - nq
            assert nq <= 12 and nk <= 6
            iq, ik = 0, 0
            pend = []

            def flush():
                qi_, n_, col_ = pend.pop(0)
                bank = gbank(GB)
                P.mm([lambda en: en.matmul(psum[bank][:, 0:n_], onesb[:], sq[:, qi_, 0:n_], start=True, stop=True)],
                     reads=[("sq", qi_), "onesb"], writes=[("ps", bank)])
                P.op("vector", lambda en: en.reduce_max(out=mx[:, gb, col_:col_ + 1], in_=psum[bank][:, 0:n_], axis=AX.X),
                     reads=[("ps", bank)], writes=[("mx", gb, col_)])

            for (apf, n, rk, isq) in srcs:
                qi = rot["sq"] % NSQ
                rot["sq"] += 1
                col = iq if isq else 12 + ik
                if isq:
                    iq += 1
                else:
                    ik += 1
                P.op("scalar", lambda en, apf=apf, qi=qi, n=n: en.activation(out=sq[:, qi, 0:n], in_=apf(), func=AF.Square), reads=rk, writes=[("sq", qi)])
                pend.append((qi, n, col))
                if len(pend) > 2:
                    flush()
                yield
            while pend:
                flush()
            yield
            P.op("vector", lambda en: en.reduce_max(out=mx[:, gb, 18:19], in_=mx[:, gb, 0:nq], axis=AX.X),
                 reads=[("mx", gb, c_) for c_ in range(nq)], writes=[("mx", gb, 18)])
            P.op("vector", lambda en: en.reduce_max(out=mx[:, gb, 19:20], in_=mx[:, gb, 12:12 + nk], axis=AX.X),
                 reads=[("mx", gb, 12 + c_) for c_ in range(nk)], writes=[("mx", gb, 19)])
            P.op("vector", lambda en: en.tensor_tensor(out=negM[:, gb:gb + 1], in0=mx[:, gb, 18:19], in1=mx[:, gb, 19:20], op=ALU.add),
                 reads=[("mx", gb, 18), ("mx", gb, 19)], writes=[("stab", gb)])
            P.op("vector", lambda en: en.tensor_scalar(out=negM[:, gb:gb + 1], in0=negM[:, gb:gb + 1], scalar1=-0.5 * SCALE, scalar2=None, op0=ALU.mult),
                 reads=[("stab", gb)], writes=[("stab", gb)])
            P.op("vector", lambda en: en.tensor_scalar(out=kbM[:, gb, :], in0=kb[:, :], scalar1=negM[:, gb:gb + 1], scalar2=None, op0=ALU.add),
                 reads=[("stab", gb), "kb"], writes=[("stabk", gb)])
            ec = j * 16 + 4 * g
            P.op("scalar", lambda en, ec=ec: en.activation(out=esk[:, gb, :], in_=esink[:, ec:ec + 4], func=AF.Exp, bias=negM[:, gb:gb + 1], scale=1.0),
                 reads=[("stab", gb), "esink"], writes=[("stabe", gb)])
            P.op("vector", lambda en: en.tensor_copy(out=eskrow[0:1, gb, :].rearrange("p (a b) -> p a b", a=4),
                                                     in_=esk[0:1, gb, :].unsqueeze(2).broadcast_to([1, 4, 128])),
                 reads=[("stabe", gb)], writes=[("stabr", gb)])
            yield

        def kvq_s(g):
            for _ in kvq(g):
                yield
            for _ in stab(g):
                yield

        def zproj(g):
            base = (j * 4 + g) * 10
            for h in range(4):
                slot = w_acquire("w_ai", base + 6 + h)
                for (s, e) in subtiles(t0, t1):
                    n = e - s
                    bank = gbank(GB)
                    proj_mm(slot, s, e, bank)
                    ta, tb = gtmp(), gtmp()
                    P.op("vector", lambda en, ta=ta, bank=bank, n=n, s=s: en.tensor_tensor(out=tmp[:, ta, 0:n], in0=psum[bank][:, 0:n],
                                                                                            in1=rstd[:, s:s + n], op=ALU.mult),
                         reads=[("ps", bank)] + rsk(s, e), writes=[("tmp", ta)])
                    P.op("scalar", lambda en, ta=ta, tb=tb, n=n: en.activation(out=tmp[:, tb, 0:n], in_=tmp[:, ta, 0:n], func=AF.Tanh, scale=0.5),
                         reads=[("tmp", ta)], writes=[("tmp", tb)])
                    P.op("vector", lambda en, ta=ta, tb=tb, n=n, s=s, e=e, h=h: en.scalar_tensor_tensor(
                        out=sz[:, h, s - t0:e - t0], in0=tmp[:, tb, 0:n], scalar=1.0, in1=tmp[:, ta, 0:n], op0=ALU.add, op1=ALU.mult),
                         reads=[("tmp", ta), ("tmp", tb)], writes=[("sz", h, b_) for b_ in range((s - t0) // 128, (e - t0) // 128)])
                w_release()

        def core(g):
            gb = g % 2
            for b in range(b0, b1):
                bl = b - b0
                kl = b - kv_lo
                qs, qe = bl * 128, (bl + 1) * 128
                pset = rot["pt"] % 2
                rot["pt"] += 1
                qkeys = [("og", 4 * g + h, bl) for h in range(4)]
                tiles = []
                if b >= 1:
                    if kl >= 1:
                        tiles.append((2, (lambda kl=kl: kT[:, gb, (kl - 1) * 128:kl * 128]), (lambda kl=kl: vv[:, gb, kl - 1, :]),
                                      [("kT", gb, kl - 1)], [("v", gb, kl - 1)], 0 * NB + b, 0))
                    else:
                        assert has_carry
                        tiles.append((2, (lambda: ck[:, g, :]), (lambda: cv[:, g, :]), [("ck", g)], [("cv", g)], 0 * NB + b, 0))
                tiles.append((3, (lambda kl=kl: kT[:, gb, kl * 128:(kl + 1) * 128]), (lambda kl=kl: vv[:, gb, kl, :]),
                              [("kT", gb, kl)], [("v", gb, kl)], 1 * NB + b, None))
                if b <= NB - 2:
                    tiles.append((4, (lambda kl=kl: kT[:, gb, (kl + 1) * 128:(kl + 2) * 128]), (lambda kl=kl: vv[:, gb, kl + 1, :]),
                                  [("kT", gb, kl + 1)], [("v", gb, kl + 1)], 2 * NB + b, 1))
                qap = lambda qs=qs, qe=qe: og[:, 4 * g:4 * g + 4, qs:qe]
                for ti, (bank, kfn, vfn, kk, vk, kbc, tri) in enumerate(tiles):
                    fns = [lambda en, bank=bank, kfn=kfn, tri=tri, qap=qap: en.matmul(psum[bank][:, :], kfn(), qap(), start=True, stop=(tri is None))]
                    rd = kk + qkeys
                    if tri is not None:
                        fns.append(lambda en, bank=bank, tri=tri: en.matmul(psum[bank][:, :], identb[:], trib[:, tri, :].unsqueeze(1).broadcast_to([128, 4, 128]),
                                                                            start=False, stop=True))
                        rd = rd + ["identb", "trib"]
                    P.mm(fns, reads=rd, writes=[("ps", bank)])
                    P.op("scalar", lambda en, bank=bank, ti=ti, kbc=kbc, pset=pset: en.activation(
                        out=PT[:, pset, ti, :], in_=psum[bank][:, :], func=AF.Exp, bias=kbM[:, gb, kbc:kbc + 1], scale=SCALE),
                         reads=[("ps", bank), ("stabk", gb)], writes=[("PT", pset, ti)])
                P.mm([lambda en, qap=qap: en.matmul(psum[7][0:16, :], kmeta[:, g, :], qap(), start=True, stop=True)],
                     reads=[("kmeta", g)] + qkeys, writes=[("ps", 7)])
                P.op("scalar", lambda en, pset=pset: en.activation(out=PT[0:16, pset, 3, :], in_=psum[7][0:16, :], func=AF.Exp, bias=negM[0:16, gb:gb + 1], scale=SCALE),
                     reads=[("ps", 7), ("stab", gb)], writes=[("PT", pset, 3)])
                yield
                nt = len(tiles)
                fns, rd = [], []
                for ti, (bank, kfn, vfn, kk, vk, kbc, tri) in enumerate(tiles):
                    fns.append(lambda en, ti=ti, vfn=vfn, pset=pset: en.matmul(psum[6][:, :], vfn(), PT[:, pset, ti, :], start=(ti == 0), stop=False))
                    rd += vk + [("PT", pset, ti)]
                fns.append(lambda en, pset=pset: en.matmul(psum[6][:, :], vmeta[0:16, g, :], PT[0:16, pset, 3, :], start=False, stop=True))
                P.mm(fns, reads=rd + [("vmeta", g), ("PT", pset, 3)], writes=[("ps", 6)])
                fns = []
                for ti in range(nt):
                    fns.append(lambda en, ti=ti, pset=pset: en.matmul(psum[7][:, :], onesb[:], PT[:, pset, ti, :], start=(ti == 0), stop=False))
                fns.append(lambda en, pset=pset: en.matmul(psum[7][:, :], onesb[0:16, :], PT[0:16, pset, 3, :], start=False, stop=False))
                fns.append(lambda en: en.matmul(psum[7][:, :], onesb[0:1, :], eskrow[0:1, gb, :], start=False, stop=True))
                P.mm(fns, reads=[("PT", pset, ti) for ti in range(nt)] + [("PT", pset, 3), "onesb", ("stabr", gb)], writes=[("ps", 7)])
                ta, tb = gtmp(), gtmp()
                ec = j * 16 + 4 * g
                P.op("vector", lambda en, ta=ta: en.reciprocal(out=tmp[:, ta, :], in_=psum[7][:, :]), reads=[("ps", 7)], writes=[("tmp", ta)])
                P.op("vector", lambda en, ta=ta, tb=tb: en.tensor_tensor(out=tmp[:, tb, :], in0=psum[6][:, :], in1=tmp[:, ta, :], op=ALU.mult),
                     reads=[("ps", 6), ("tmp", ta)], writes=[("tmp", tb)])
                P.op("vector", lambda en, tb=tb, qs=qs, qe=qe: en.scalar_tensor_tensor(
                    out=og[:, 4 * g:4 * g + 4, qs:qe], in0=tmp[:, tb, :].rearrange("p (a b) -> p a b", a=4), scalar=0.5,
                    in1=sz[:, :, qs:qe], op0=ALU.mult, op1=ALU.mult),
                     reads=[("tmp", tb)] + [("sz", h, bl) for h in range(4)], writes=qkeys)
                yield
            if has_next:
                lb = b1 - 1 - kv_lo
                P.op("scalar", lambda en, lb=lb: en.activation(out=ck[:, g, :], in_=kT[:, gb, lb * 128:(lb + 1) * 128], func=AF.Copy),
                     reads=[("kT", gb, lb)], writes=[("ck", g)])
                P.op("scalar", lambda en, lb=lb: en.activation(out=cv[:, g, :], in_=vv[:, gb, lb, :], func=AF.Copy),
                     reads=[("v", gb, lb)], writes=[("cv", g)])

        def core_pipe(g):
            gb = g % 2
            sets = [[2, 3, 4], [0, 1, 5]]
            st = {}

            def scores(b):
                bl, kl = b - b0, b - kv_lo
                qs, qe = bl * 128, (bl + 1) * 128
                bs = sets[bl % 2]
                pset = bl % 2
                qkeys = [("og", 4 * g + h, bl) for h in range(4)]
                tiles = []
                if b >= 1:
                    if kl >= 1:
                        tiles.append((bs[0], (lambda kl=kl: kT[:, gb, (kl - 1) * 128:kl * 128]), (lambda kl=kl: vv[:, gb, kl - 1, :]),
                                      [("kT", gb, kl - 1)], [("v", gb, kl - 1)], 0 * NB + b, 0))
                    else:
                        assert has_carry
                        tiles.append((bs[0], (lambda: ck[:, g, :]), (lambda: cv[:, g, :]), [("ck", g)], [("cv", g)], 0 * NB + b, 0))
                tiles.append((bs[1], (lambda kl=kl: kT[:, gb, kl * 128:(kl + 1) * 128]), (lambda kl=kl: vv[:, gb, kl, :]),
                              [("kT", gb, kl)], [("v", gb, kl)], 1 * NB + b, None))
                if b <= NB - 2:
                    tiles.append((bs[2], (lambda kl=kl: kT[:, gb, (kl + 1) * 128:(kl + 2) * 128]), (lambda kl=kl: vv[:, gb, kl + 1, :]),
                                  [("kT", gb, kl + 1)], [("v", gb, kl + 1)], 2 * NB + b, 1))
                qap = lambda qs=qs, qe=qe: og[:, 4 * g:4 * g + 4, qs:qe]
                for ti, (bank, kfn, vfn, kk, vk, kbc, tri) in enumerate(tiles):
                    fns = [lambda en, bank=bank, kfn=kfn, tri=tri, qap=qap: en.matmul(psum[bank][:, :], kfn(), qap(), start=True, stop=(tri is None))]
                    rd = kk + qkeys
                    if tri is not None:
                        fns.append(lambda en, bank=bank, tri=tri: en.matmul(psum[bank][:, :], identb[:], trib[:, tri, :].unsqueeze(1).broadcast_to([128, 4, 128]),
                                                                            start=False, stop=True))
                        rd = rd + ["identb", "trib"]
                    P.mm(fns, reads=rd, writes=[("ps", bank)])
                    P.op("scalar", lambda en, bank=bank, ti=ti, kbc=kbc, pset=pset: en.activation(
                        out=PT[:, pset, ti, :], in_=psum[bank][:, :], func=AF.Exp, bias=kbM[:, gb, kbc:kbc + 1], scale=SCALE),
                         reads=[("ps", bank), ("stabk", gb)], writes=[("PT", pset, ti)])
                st[b] = (tiles, qap, qkeys, pset, qs, qe, bl)

            def meta_scores(b):
                tiles, qap, qkeys, pset, qs, qe, bl = st[b]
                P.mm([lambda en, qap=qap: en.matmul(psum[7][0:16, :], kmeta[:, g, :], qap(), start=True, stop=True)],
                     reads=[("kmeta", g)] + qkeys, writes=[("ps", 7)])
                P.op("scalar", lambda en, pset=pset: en.activation(out=PT[0:16, pset, 3, :], in_=psum[7][0:16, :], func=AF.Exp, bias=negM[0:16, gb:gb + 1], scale=SCALE),
                     reads=[("ps", 7), ("stab", gb)], writes=[("PT", pset, 3)])

            def finish(b):
                tiles, qap, qkeys, pset, qs, qe, bl = st.pop(b)
                nt = len(tiles)
                fns, rd = [], []
                for ti, (bank, kfn, vfn, kk, vk, kbc, tri) in enumerate(tiles):
                    fns.append(lambda en, ti=ti, vfn=vfn, pset=pset: en.matmul(psum[6][:, :], vfn(), PT[:, pset, ti, :], start=(ti == 0), stop=False))
                    rd += vk + [("PT", pset, ti)]
                fns.append(lambda en, pset=pset: en.matmul(psum[6][:, :], vmeta[0:16, g, :], PT[0:16, pset, 3, :], start=False, stop=True))
                P.mm(fns, reads=rd + [("vmeta", g), ("PT", pset, 3)], writes=[("ps", 6)])
                fns = []
                for ti in range(nt):
                    fns.append(lambda en, ti=ti, pset=pset: en.matmul(psum[7][:, :], onesb[:], PT[:, pset, ti, :], start=(ti == 0), stop=False))
                fns.append(lambda en, pset=pset: en.matmul(psum[7][:, :], onesb[0:16, :], PT[0:16, pset, 3, :], start=False, stop=False))
                fns.append(lambda en: en.matmul(psum[7][:, :], onesb[0:1, :], eskrow[0:1, gb, :], start=False, stop=True))
                P.mm(fns, reads=[("PT", pset, ti) for ti in range(nt)] + [("PT", pset, 3), "onesb", ("stabr", gb)], writes=[("ps", 7)])
                ta, tb = gtmp(), gtmp()
                ec = j * 16 + 4 * g
                P.op("vector", lambda en, ta=ta: en.reciprocal(out=tmp[:, ta, :], in_=psum[7][:, :]), reads=[("ps", 7)], writes=[("tmp", ta)])
                P.op("vector", lambda en, ta=ta, tb=tb: en.tensor_tensor(out=tmp[:, tb, :], in0=psum[6][:, :], in1=tmp[:, ta, :], op=ALU.mult),
                     reads=[("ps", 6), ("tmp", ta)], writes=[("tmp", tb)])
                P.op("vector", lambda en, tb=tb, qs=qs, qe=qe: en.scalar_tensor_tensor(
                    out=og[:, 4 * g:4 * g + 4, qs:qe], in0=tmp[:, tb, :].rearrange("p (a b) -> p a b", a=4), scalar=0.5,
                    in1=sz[:, :, qs:qe], op0=ALU.mult, op1=ALU.mult),
                     reads=[("tmp", tb)] + [("sz", h, bl) for h in range(4)], writes=qkeys)

            scores(b0)
            meta_scores(b0)
            for b in range(b0, b1):
                if b + 1 < b1:
                    scores(b + 1)
                finish(b)
                if b + 1 < b1:
                    meta_scores(b + 1)
            if has_next:
                lb = b1 - 1 - kv_lo
                P.op("scalar", lambda en, lb=lb: en.activation(out=ck[:, g, :], in_=kT[:, gb, lb * 128:(lb + 1) * 128], func=AF.Copy),
                     reads=[("kT", gb, lb)], writes=[("ck", g)])
                P.op("scalar", lambda en, lb=lb: en.activation(out=cv[:, g, :], in_=vv[:, gb, lb, :], func=AF.Copy),
                     reads=[("v", gb, lb)], writes=[("cv", g)])

        def run(gen):
            for _ in gen:
                pass

        def interleave(main, filler, n_main, n_fill):
            done_f = 0
            fill_alive = True
            for i, _ in enumerate(main):
                want = ((i + 1) * n_fill + n_main - 1) // n_main
                while fill_alive and done_f < want:
                    try:
                        next(filler)
                        done_f += 1
                    except StopIteration:
                        fill_alive = False
            if fill_alive:
                run(filler)

        n_core = 2 * (b1 - b0)
        n_kvq = len(subtiles(r0, r1)) + nblk + 4 * len(subtiles(t0, t1)) + (2 if first else 0) \
            + 4 * len(subtiles(t0, t1)) + len(subtiles(0, nr)) + 3
        run(kvq_s(0))
        zproj(0)
        for g in range(4):
            if g < 3:
                interleave(core(g), kvq_s(g + 1), n_core, n_kvq)
                zproj(g + 1)
            else:
                core_pipe(g)
        out_proj(l, "w_ao", j, t0, t1)

    def conv_tile(seg, l, b0, b1, has_carry, has_next):
        j = l // 2
        t0, t1 = b0 * 128, b1 * 128
        T = t1 - t0
        left_ext = (not has_carry) and b0 > 0
        r0 = (b0 - 1) * 128 if left_ext else t0
        r1 = min(b1 + 1, NB) * 128
        xs_ = t0 - 1 if left_ext else t0
        xe = min(t1 + 1, L)
        nr = r1 - r0
        P.dma("sync", lambda e: e.dma_start(out=cosr[:, 0:nr], in_=okd[seg, r0:r1].partition_broadcast(128)), "cos", writes=["cosr"])
        P.op("vector", lambda e: e.tensor_tensor(out=cosr[:, 0:nr], in0=cosr[:, 0:nr], in1=rstd[:, r0:r1], op=ALU.mult),
             reads=["cosr"] + rsk(r0, r1), writes=["cosr"])
        P.op("vector", lambda e: e.tensor_tensor(out=cosr[:, 0:nr], in0=cosr[:, 0:nr], in1=rstd[:, r0:r1], op=ALU.mult),
             reads=["cosr"] + rsk(r0, r1), writes=["cosr"])
        xsubs = subtiles(xs_, xe)
        for f in range(KC):
            base = (j * KC + f) * 4
            cb = f % 2
            if not left_ext:
                if b0 == 0:
                    P.op("vector", lambda en, cb=cb: en.memset(cu[:, cb, 0:1], 0.0), writes=[("cu", cb, "l")])
                else:
                    P.op("vector", lambda en, cb=cb, f=f: en.tensor_copy(out=cu[:, cb, 0:1], in_=ccu[:, f:f + 1]), reads=[("ccu", f)], writes=[("cu", cb, "l")])
            if xe == t1:
                P.op("vector", lambda en, cb=cb: en.memset(cu[:, cb, T + 1:T + 2], 0.0), writes=[("cu", cb, "r")])
            slot_c = w_acquire("w_ci", base + 1)
            csb = []
            for (s, e) in xsubs:
                n = e - s
                bank = gbank(ALLB)
                proj_mm(slot_c, s, e, bank)
                ta = gtmp()
                P.op("scalar", lambda en, ta=ta, bank=bank, n=n: en.activation(out=tmp[:, ta, 0:n], in_=psum[bank][:, 0:n], func=AF.Copy),
                     reads=[("ps", bank)], writes=[("tmp", ta)])
                P.op("vector", lambda en, ta=ta, n=n, s=s: en.tensor_tensor(out=tmp[:, ta, 0:n], in0=tmp[:, ta, 0:n], in1=cosr[:, s - r0:s - r0 + n], op=ALU.mult),
                     reads=[("tmp", ta), "cosr"], writes=[("tmp", ta)])
                csb.append(ta)
            w_release()
            assert len(csb) <= NTMP - 2
            slot_u = w_acquire("w_ci", base + 2)
            for si, (s, e) in enumerate(xsubs):
                n = e - s
                bank = gbank(ALLB)
                proj_mm(slot_u, s, e, bank)
                ta = csb[si]
                P.op("vector", lambda en, ta=ta, bank=bank, n=n, s=s, cb=cb: en.tensor_tensor(
                    out=cu[:, cb, 1 + s - t0:1 + s - t0 + n], in0=psum[bank][:, 0:n], in1=tmp[:, ta, 0:n], op=ALU.mult),
                     reads=[("ps", bank), ("tmp", ta)], writes=[("cu", cb, si)])
            w_release()
            cukeys = [("cu", cb, "l"), ("cu", cb, "r")] + [("cu", cb, si) for si in range(len(xsubs))]
            slot_b = w_acquire("w_ci", base + 0)
            for si, (s, e) in enumerate(subtiles(t0, t1)):
                n = e - s
                bank = gbank(ALLB)
                proj_mm(slot_b, s, e, bank)
                P.op("vector", lambda en, bank=bank, n=n, s=s, cb=cb: en.tensor_tensor(
                    out=gate[:, cb, s - t0:s - t0 + n], in0=psum[bank][:, 0:n], in1=rstd[:, s:s + n], op=ALU.mult),
                     reads=[("ps", bank)] + rsk(s, e), writes=[("gate", cb, si)])
            w_release()
            slot_z = w_acquire("w_ci", base + 3)
            for si, (s, e) in enumerate(subtiles(t0, t1)):
                n = e - s
                bank = gbank(ALLB)
                proj_mm(slot_z, s, e, bank)
                ta, tb = gtmp(), gtmp()
                P.op("vector", lambda en, ta=ta, bank=bank, n=n, s=s: en.tensor_tensor(out=tmp[:, ta, 0:n], in0=psum[bank][:, 0:n],
                                                                                        in1=rstd[:, s:s + n], op=ALU.mult),
                     reads=[("ps", bank)] + rsk(s, e), writes=[("tmp", ta)])
                P.op("scalar", lambda en, ta=ta, tb=tb, n=n: en.activation(out=tmp[:, tb, 0:n], in_=tmp[:, ta, 0:n], func=AF.Tanh, scale=0.5),
                     reads=[("tmp", ta)], writes=[("tmp", tb)])
                P.op("vector", lambda en, ta=ta, tb=tb, n=n: en.scalar_tensor_tensor(
                    out=tmp[:, tb, 0:n], in0=tmp[:, tb, 0:n], scalar=1.0, in1=tmp[:, ta, 0:n], op0=ALU.add, op1=ALU.mult),
                     reads=[("tmp", ta), ("tmp", tb)], writes=[("tmp", tb)])
                P.op("vector", lambda en, tb=tb, n=n, s=s, cb=cb: en.tensor_tensor(
                    out=gate[:, cb, s - t0:s - t0 + n], in0=gate[:, cb, s - t0:s - t0 + n], in1=tmp[:, tb, 0:n], op=ALU.mult),
                     reads=[("gate", cb, si), ("tmp", tb)], writes=[("gate", cb, si)])
            w_release()
            wc = (j * 3) * KC + f
            P.op("vector", lambda en, cb=cb, wc=wc: en.tensor_scalar(out=ybuf[:, 0:T], in0=cu[:, cb, 0:T], scalar1=cw[:, wc:wc + 1], scalar2=None, op0=ALU.mult),
                 reads=cukeys + ["cw"], writes=["ybuf"])
            P.op("vector", lambda en, cb=cb, wc=wc: en.scalar_tensor_tensor(out=ybuf[:, 0:T], in0=cu[:, cb, 1:T + 1], scalar=cw[:, wc + KC:wc + KC + 1],
                                                                            in1=ybuf[:, 0:T], op0=ALU.mult, op1=ALU.add),
                 reads=cukeys + ["cw", "ybuf"], writes=["ybuf"])
            P.op("vector", lambda en, cb=cb, wc=wc: en.scalar_tensor_tensor(out=ybuf[:, 0:T], in0=cu[:, cb, 2:T + 2], scalar=cw[:, wc + 2 * KC:wc + 2 * KC + 1],
                                                                            in1=ybuf[:, 0:T], op0=ALU.mult, op1=ALU.add),
                 reads=cukeys + ["cw", "ybuf"], writes=["ybuf"])
            nsub = len(subtiles(t0, t1))
            P.op("vector", lambda en, cb=cb, f=f: en.scalar_tensor_tensor(out=og[:, f, 0:T], in0=ybuf[:, 0:T], scalar=0.5, in1=gate[:, cb, 0:T],
                                                                          op0=ALU.mult, op1=ALU.mult),
                 reads=["ybuf"] + [("gate", cb, si) for si in range(nsub)], writes=[("og", f, b_) for b_ in range(b1 - b0)])
            if has_next:
                P.op("vector", lambda en, cb=cb, f=f: en.tensor_copy(out=ccu[:, f:f + 1], in_=cu[:, cb, T:T + 1]), reads=cukeys, writes=[("ccu", f)])
        out_proj(l, "w_co", j, t0, t1)

    def load_segment(seg):
        P.dma("sync", lambda e: e.dma_start(out=kb[:], in_=kbd[seg]), "kb", writes=["kb"])
        for blk in range(NB):
            xb = blk % NXIN
            P.dma("sync", lambda e, blk=blk, xb=xb: e.dma_start(out=xin[:, xb, :], in_=xs[seg, blk]), ("x", xb), writes=[("xin", xb)])
            hk = [("hi", kc, blk) for kc in range(KC)]
            assert not USE_LO
            P.op("vector", lambda en, blk=blk, xb=xb: en.tensor_tensor(
                out=hi[:, :, blk * 128:(blk + 1) * 128], in0=xin[:, xb, :].rearrange("p (a b) -> p a b", a=KC),
                in1=gv[:, 0:KC].unsqueeze(2).broadcast_to([128, KC, 128]), op=ALU.mult),
                 reads=[("xin", xb), "gv"], writes=hk)
            bank = gbank(ALLB)
            sqk = [("sq", q_) for q_ in range(4)]
            P.op("scalar", lambda en, xb=xb: en.activation(out=sq[:, 0:4, :], in_=xin[:, xb, :].rearrange("p (a b) -> p a b", a=4), func=AF.Square),
                 reads=[("xin", xb)], writes=sqk)
            fns = [lambda en, kc=kc, bank=bank: en.matmul(psum[bank][:, 0:128], onesb[:], sq[:, kc // 4, (kc % 4) * 128:(kc % 4 + 1) * 128],
                                                         start=(kc == 0), stop=(kc == KC - 1)) for kc in range(KC)]
            P.mm(fns, reads=sqk + ["onesb"], writes=[("ps", bank)])
            t = gtmp()
            P.op("scalar", lambda en, t=t, bank=bank: en.activation(out=tmp[:, t, 0:128], in_=psum[bank][:, 0:128], func=AF.Sqrt, bias=epst[:, 0:1], scale=1.0 / D),
                 reads=[("ps", bank), "epst"], writes=[("tmp", t)])
            P.op("vector", lambda en, t=t, blk=blk: en.reciprocal(out=rstd[:, blk * 128:(blk + 1) * 128], in_=tmp[:, t, 0:128]),
                 reads=[("tmp", t)], writes=[("rs", blk)])

    def store_segment(seg):
        for oi, blk in enumerate(OUT_BLOCKS[seg]):
            hk_all = [("hi", kc, blk) for kc in range(KC)]
            P.op("vector", lambda en, blk=blk: en.tensor_tensor(
                out=hf, in0=hi[:, :, blk * 128:(blk + 1) * 128], in1=rstd[:, blk * 128:(blk + 1) * 128].unsqueeze(1).broadcast_to([128, KC, 128]), op=ALU.mult),
                 reads=hk_all + rsk(blk * 128, (blk + 1) * 128), writes=[("xin", 2)])
            xb = rot["x"] % 2
            rot["x"] += 1
            P.op("vector", lambda en, xb=xb: en.tensor_tensor(
                out=xin[:, xb, :].rearrange("p (a b) -> p a b", a=KC), in0=hf, in1=gft[:, :].unsqueeze(2).broadcast_to([128, KC, 128]), op=ALU.mult),
                 reads=[("xin", 2), "gft"], writes=[("xin", xb)])
            P.dma("sync", lambda e, oi=oi, xb=xb: e.dma_start(out=outd[seg][oi], in_=xin[:, xb, :]), ("o", xb), reads=[("xin", xb)])

    w_issue(NSLOT - 1)
    for seg in range(2):
        load_segment(seg)
        for l in range(N_LAYERS):
            P.barrier()
            tl = SEG_TILES[seg][l]
            for ti_, (b0, b1) in enumerate(tl):
                has_carry = ti_ > 0 and tl[ti_ - 1][1] == b0
                has_next = ti_ + 1 < len(tl) and tl[ti_ + 1][0] == b1
                if l % 2 == 0:
                    attn_tile(seg, l, b0, b1, has_carry, has_next, ti_ == 0)
                else:
                    conv_tile(seg, l, b0, b1, has_carry, has_next)
        P.barrier()
        store_segment(seg)
    assert wstate["next"] == len(wsched)
    P.wait_all("sync", P.all_events())
    P.emit()
    P.close()
    for cm in reversed(ctx):
        cm.__exit__(None, None, None)
    return nc


def _segment_meta(core):
    res = []
    pos = np.arange(L, dtype=np.int64) - 112
    res.append((pos, pos >= 0, list(range(NB)), "s"))
    real = [0, 1, 2] + [8 * core - 2 + i for i in range(14)]
    posp = np.zeros(L, dtype=np.int64)
    okp = np.zeros(L, dtype=bool)
    for vb, r in enumerate(real):
        sl = slice(vb * 128, (vb + 1) * 128)
        if 0 <= r <= 64:
            p = r * 128 + np.arange(128) - 112
            posp[sl] = p
            okp[sl] = p >= 0
        else:
            posp[sl] = -10 ** 7 - vb * 1000 - np.arange(128)
            okp[sl] = False
    res.append((posp, okp, real, "p"))
    return res


def _tables(pos, ok, lead_blocks):
    half = HEAD // 2
    inv_freq = (np.float32(10000.0) ** (-np.arange(0, half, dtype=np.float32) * np.float32(2.0 / HEAD))).astype(np.float32)
    posf = np.where(ok, pos, 0).astype(np.float32)
    ang = posf[:, None] * inv_freq[None, :]
    c = np.cos(ang).astype(np.float32).T
    s = np.sin(ang).astype(np.float32).T
    cosT = np.concatenate([c, c], 0)
    sinT = np.concatenate([s, -s], 0)
    kbt = np.full((128, 3 * NB), NEGB, dtype=np.float32)
    ii = np.arange(128)
    tri = {0: (ii[None, :] <= ii[:, None]), 1: np.ones((128, 128), bool), 2: (ii[:, None] <= ii[None, :])}
    for b in range(NB):
        pq = pos[b * 128:(b + 1) * 128]
        for d in range(3):
            kbk = b + d - 1
            if kbk < 0 or kbk >= NB:
                continue
            pk = pos[kbk * 128:(kbk + 1) * 128]
            okk = ok[kbk * 128:(kbk + 1) * 128].copy()
            if kbk in lead_blocks:
                okk[:] = False
            allowed = okk[:, None] & (np.abs(pq[None, :] - pk[:, None]) <= 128)
            flag = allowed.any(axis=1)
            mine = tri[d] & flag[:, None]
            qreal = ok[b * 128:(b + 1) * 128]
            assert np.array_equal(mine[:, qreal], allowed[:, qreal]), ("mask mismatch", b, d)
            kbt[flag, d * NB + b] = 0.0
    return cosT, sinT, kbt


_PREP_CACHE = {}


def kernel(x_prompt, x_sample, meta_tokens, norm_w, attn_w_in, attn_w_out, attn_sink, conv_w_in, conv_w, conv_w_out, final_norm_w):
    f32 = np.float32
    x_prompt = np.asarray(x_prompt, f32)
    x_sample = np.asarray(x_sample, f32)
    meta_tokens = np.asarray(meta_tokens, f32)
    lead = np.concatenate([np.zeros((112, D), f32), meta_tokens], 0)
    zeros_blk = np.zeros((128, D), f32)

    def chunked(w):
        n = w.shape[1] // 128
        return np.ascontiguousarray(w.reshape(KC, 128, n, 128).transpose(2, 1, 0, 3)).reshape(n, 128, D)

    attn_w_in = np.asarray(attn_w_in, f32)
    attn_w_out = np.asarray(attn_w_out, f32)
    conv_w_in = np.asarray(conv_w_in, f32)
    conv_w_out = np.asarray(conv_w_out, f32)
    w_ai = np.empty((2, 4, 10, 128, D), f32)
    for j in range(2):
        ch = chunked(attn_w_in[j])
        for g in range(4):
            idx = [4 * g + h for h in range(4)] + [16 + g, 20 + g] + [24 + 4 * g + h for h in range(4)]
            w_ai[j, g] = ch[idx]
    w_ai = w_ai.reshape(80, 128, D)
    w_ao = np.stack([chunked(attn_w_out[j]) for j in range(2)]).reshape(32, 128, D)
    w_ci = np.empty((2, KC, 4, 128, D), f32)
    for j in range(2):
        ch = chunked(conv_w_in[j])
        for f in range(KC):
            w_ci[j, f] = ch[[f, 16 + f, 32 + f, 48 + f]]
    w_ci = w_ci.reshape(128, 128, D)
    w_co = np.stack([chunked(conv_w_out[j]) for j in range(2)]).reshape(32, 128, D)

    norm_w = np.asarray(norm_w, f32)
    gvd = np.ascontiguousarray(norm_w.reshape(4, KC, 128).transpose(2, 0, 1)).reshape(128, 64)
    gfd = np.ascontiguousarray(np.asarray(final_norm_w, f32).reshape(KC, 128).T)
    skd = np.asarray(attn_sink, f32).reshape(32)
    cwd = np.ascontiguousarray(np.asarray(conv_w, f32).reshape(2, 3, KC, 128).transpose(3, 0, 1, 2)).reshape(128, 96)
    idd = np.eye(128, dtype=f32)
    ii = np.arange(128)
    tri_prev = np.where(ii[None, :] <= ii[:, None], 0.0, NEGB).astype(f32)
    tri_next = np.where(ii[:, None] <= ii[None, :], 0.0, NEGB).astype(f32)
    trid = np.concatenate([tri_prev, tri_next], 1)

    in_maps = []
    for c in range(NCORES):
        metas = _segment_meta(c)
        xs = np.empty((2, NB, 128, D), f32)
        xs[0, 0] = lead
        xs[0, 1:] = x_sample[c].reshape(16, 128, D)
        for vb, r in enumerate(metas[1][2]):
            if r == 0:
                xs[1, vb] = lead
            elif 1 <= r <= 64:
                xs[1, vb] = x_prompt[0, (r - 1) * 128:r * 128]
            else:
                xs[1, vb] = zeros_blk
        xs = np.ascontiguousarray(xs.reshape(2, NB, 128, KC, 128).transpose(0, 1, 4, 3, 2)).reshape(2, NB, 128, D)
        cosd = np.empty((2, 128, L), f32)
        sind = np.empty((2, 128, L), f32)
        okd = np.empty((2, L), f32)
        kbd = np.empty((2, 128, 3 * NB), f32)
        for sgi, (pos, ok, _real, _k) in enumerate(metas):
            key = (sgi, c if sgi == 1 else 0)
            if key not in _PREP_CACHE:
                _PREP_CACHE[key] = _tables(pos, ok, {vb for vb, r in enumerate(_real) if r == 0})
            cosd[sgi], sind[sgi], kbd[sgi] = _PREP_CACHE[key]
            okd[sgi] = ok.astype(f32)
        in_maps.append({"xs": xs, "cosd": cosd, "sind": sind, "okd": okd, "kbd": kbd, "w_ai": w_ai, "w_ao": w_ao,
                        "w_ci": w_ci, "w_co": w_co, "gvd": gvd, "gfd": gfd, "skd": skd, "cwd": cwd, "idd": idd, "trid": trid})

    nc = build_nc()
    res = run_bass_kernel_spmd(nc, in_maps, core_ids=list(range(NCORES)))
    def tok_major(a):
        a = np.asarray(a, f32)
        n = a.shape[0]
        return np.ascontiguousarray(a.reshape(n, 128, KC, 128).transpose(0, 3, 2, 1)).reshape(n * 128, D)

    y_sample = np.stack([tok_major(res.results[c]["ys"]) for c in range(NCORES)], 0)
    y_prompt = np.concatenate([tok_major(res.results[c]["yp"]) for c in range(NCORES)], 0).reshape(1, 8192, D)
    return (y_prompt, y_sample)
```

```python
import numpy as np
import ml_dtypes
import concourse.bass as bass
import concourse.mybir as mybir
from concourse.bass_utils import run_bass_kernel_spmd

F32 = mybir.dt.float32
BF16 = mybir.dt.bfloat16
AF = mybir.ActivationFunctionType
ALU = mybir.AluOpType
AX = mybir.AxisListType

D = 2048
KC = 16
NB = 17
L = NB * 128
NCORES = 8
HEAD = 128
SCALE = HEAD ** -0.5
EPS = 1e-6
NEGB = -30000.0
SEG_TILES = {
    0: [[(0, 9), (9, 17)], [(0, 9), (9, 17)], [(0, 9), (9, 17)], [(1, 9), (9, 17)]],
    1: [[(0, 9), (9, 17)], [(0, 9), (9, 16)], [(5, 15)], [(6, 14)]],
}
TMAX = max(b1 - b0 for sg in SEG_TILES.values() for tl in sg for b0, b1 in tl) * 128
RMAX = TMAX + 256
NSLOT = 4
USE_LO = False
OUT_BLOCKS = {0: list(range(1, 17)), 1: list(range(6, 14))}
N_LAYERS = 4

ENGINES = ("tensor", "vector", "scalar", "gpsimd", "sync")


class Prog:
    def __init__(self, nc):
        self.nc = nc
        self.ops = {e: [] for e in ENGINES}
        self.cnt = {e: 0 for e in ENGINES}
        self.seen = {e: {} for e in ENGINES}
        self.last_w = {}
        self.readers = {}
        self.dma_sems = {}
        self.sem_handles = {}
        self._ctx = []
        for e in ENGINES:
            cm = nc.semaphore("s_" + e)
            h = cm.__enter__()
            self._ctx.append(cm)
            self.sem_handles[("eng", e)] = h

    def dma_sem(self, key):
        if key not in self.dma_sems:
            cm = self.nc.semaphore("d_" + "".join(ch if ch.isalnum() else "_" for ch in str(key)))
            h = cm.__enter__()
            self._ctx.append(cm)
            self.dma_sems[key] = [h, 0]
            self.sem_handles[("dma", key)] = h
        return self.dma_sems[key]

    def _deps(self, reads, writes):
        ev = []
        for r in reads:
            w = self.last_w.get(r)
            if w is not None:
                ev.append(w)
        for w_ in writes:
            w = self.last_w.get(w_)
            if w is not None:
                ev.append(w)
            ev.extend(self.readers.get(w_, ()))
        return ev

    def _waits(self, eng, events):
        need = {}
        seen = self.seen[eng]
        for (k, v) in events:
            if seen.get(k, 0) >= v:
                continue
            if need.get(k, 0) < v:
                need[k] = v
        for k, v in need.items():
            seen[k] = v
        return list(need.items())

    def _record(self, reads, writes, event):
        for r in reads:
            self.readers.setdefault(r, []).append(event)
        for w in writes:
            self.last_w[w] = event
            self.readers[w] = []

    def op(self, eng, fn, reads=(), writes=()):
        waits = self._waits(eng, self._deps(reads, writes))
        self.cnt[eng] += 1
        event = (("eng", eng), self.cnt[eng])
        self.ops[eng].append((waits, fn, [(("eng", eng), 1)]))
        self._record(reads, writes, event)
        return event

    def mm(self, fns, reads=(), writes=()):
        eng = "tensor"
        waits = self._waits(eng, self._deps(reads, writes))
        self.cnt[eng] += 1
        event = (("eng", eng), self.cnt[eng])
        n = len(fns)
        for i, fn in enumerate(fns):
            self.ops[eng].append((waits if i == 0 else [], fn, [(("eng", eng), 1)] if i == n - 1 else []))
        self._record(reads, writes, event)
        return event

    def dma(self, eng, fn, semkey, reads=(), writes=()):
        waits = self._waits(eng, self._deps(reads, writes))
        s = self.dma_sem(semkey)
        s[1] += 16
        event = (("dma", semkey), s[1])
        self.ops[eng].append((waits, fn, [(("dma", semkey), 16)]))
        self._record(reads, writes, event)
        return event

    def barrier(self):
        evs = self.all_events()
        for e in ENGINES:
            self.wait_all(e, evs)

    def wait_all(self, eng, events):
        waits = self._waits(eng, list(events))
        if waits:
            self.ops[eng].append((waits, None, []))

    def all_events(self):
        evs = [(("eng", e), self.cnt[e]) for e in ENGINES if self.cnt[e] > 0]
        evs += [(("dma", k), v[1]) for k, v in self.dma_sems.items() if v[1] > 0]
        return evs

    def emit(self):
        nc = self.nc
        handles = self.sem_handles
        with nc.Block() as block:
            for e in ENGINES:
                ops = self.ops[e]
                if not ops:
                    continue

                def body(engobj, ops=ops):
                    for waits, fn, incs in ops:
                        for k, v in waits:
                            engobj.wait_ge(handles[k], v)
                        if fn is None:
                            continue
                        ins = fn(engobj)
                        for k, n in incs:
                            ins.then_inc(handles[k], n)

                getattr(block, e)(body)

    def close(self):
        for cm in reversed(self._ctx):
            cm.__exit__(None, None, None)


def subtiles(s, e, n=512):
    ln = e - s
    if ln <= 0:
        return []
    k = (ln + n - 1) // n
    size = min(n, ((ln + k - 1) // k + 127) // 128 * 128)
    out = []
    while s < e:
        out.append((s, min(s + size, e)))
        s += size
    return out


def weight_schedule():
    sched = []
    for seg in range(2):
        for l in range(N_LAYERS):
            j = l // 2
            for _tile in SEG_TILES[seg][l]:
                if l % 2 == 0:
                    for g in range(4):
                        for c in (4, 5, 0, 1, 2, 3, 6, 7, 8, 9):
                            sched.append(("w_ai", (j * 4 + g) * 10 + c))
                    for m in range(KC):
                        sched.append(("w_ao", j * KC + m))
                else:
                    for f in range(KC):
                        for c in (1, 2, 0, 3):
                            sched.append(("w_ci", (j * KC + f) * 4 + c))
                    for m in range(KC):
                        sched.append(("w_co", j * KC + m))
    return sched


def build_nc():
    nc = bass.Bass("TRN2", target_bir_lowering=False)
    dt_in = lambda n, s: nc.dram_tensor(n, s, F32, kind="ExternalInput").ap()
    xs = dt_in("xs", [2, NB, 128, D])
    cosd = dt_in("cosd", [2, 128, L])
    sind = dt_in("sind", [2, 128, L])
    okd = dt_in("okd", [2, L])
    kbd = dt_in("kbd", [2, 128, 3 * NB])
    wd = {
        "w_ai": dt_in("w_ai", [2 * 4 * 10, 128, D]),
        "w_ao": dt_in("w_ao", [2 * KC, 128, D]),
        "w_ci": dt_in("w_ci", [2 * KC * 4, 128, D]),
        "w_co": dt_in("w_co", [2 * KC, 128, D]),
    }
    gvd = dt_in("gvd", [128, 64])
    gfd = dt_in("gfd", [128, KC])
    skd = dt_in("skd", [32])
    cwd = dt_in("cwd", [128, 96])
    idd = dt_in("idd", [128, 128])
    trid = dt_in("trid", [128, 256])
    ysd = nc.dram_tensor("ys", [16, 128, D], F32, kind="ExternalOutput").ap()
    ypd = nc.dram_tensor("yp", [8, 128, D], F32, kind="ExternalOutput").ap()
    outd = {0: ysd, 1: ypd}

    P = Prog(nc)
    ctx = []

    def sb(name, shape, dt):
        cm = nc.sbuf_tensor(name, shape, dt)
        t = cm.__enter__()
        ctx.append(cm)
        return t

    hi = sb("hi", [128, KC, L], BF16)
    lo = sb("lo", [128, KC, L], BF16) if USE_LO else None
    scrA = sb("scrA", [128, KC * TMAX], BF16)
    og = scrA[:, :].rearrange("p (a b) -> p a b", a=KC)
    NXIN = 5
    xin = scrA[:, 0:NXIN * 4096].bitcast(F32).rearrange("p (a b) -> p a b", a=NXIN)
    hf = scrA[:, 8192:12288].bitcast(F32).rearrange("p (a b) -> p a b", a=KC)
    assert KC * TMAX >= NXIN * 4096
    assert KC * TMAX >= 16384
    o_sz, o_kT, o_vv = 0, 4 * TMAX, 4 * TMAX + 2 * RMAX
    o_ck = o_vv + 2 * RMAX
    o_cv, o_km, o_vm = o_ck + 512, o_ck + 1024, o_ck + 1024 + 64
    o_pt = o_vm + 512
    o_sin = o_pt + 4096
    nB = o_sin + 2 * RMAX
    assert nB >= 2 * (2 * (TMAX + 2)) + 2 * 2 * TMAX + 2 * TMAX
    scrB = sb("scrB", [128, nB], BF16)
    sz = scrB[:, o_sz:o_sz + 4 * TMAX].rearrange("p (a b) -> p a b", a=4)
    kT = scrB[:, o_kT:o_kT + 2 * RMAX].rearrange("p (a b) -> p a b", a=2)
    vv = scrB[:, o_vv:o_vv + 2 * RMAX].rearrange("p (a c b) -> p a c b", a=2, b=128)
    ck = scrB[:, o_ck:o_ck + 512].rearrange("p (a b) -> p a b", a=4)
    cv = scrB[:, o_cv:o_cv + 512].rearrange("p (a b) -> p a b", a=4)
    kmeta = scrB[:, o_km:o_km + 64].rearrange("p (a b) -> p a b", a=4)
    vmeta = scrB[:, o_vm:o_vm + 512].rearrange("p (a b) -> p a b", a=4)
    PT = scrB[:, o_pt:o_pt + 4096].rearrange("p (a b c) -> p a b c", a=2, b=4)
    sinr = scrB[:, o_sin:o_sin + 2 * RMAX].bitcast(F32)
    c0 = 2 * 2 * (TMAX + 2)
    cu = scrB[:, 0:c0].bitcast(F32).rearrange("p (a b) -> p a b", a=2)
    gate = scrB[:, c0:c0 + 4 * TMAX].bitcast(F32).rearrange("p (a b) -> p a b", a=2)
    ybuf = scrB[:, c0 + 4 * TMAX:c0 + 6 * TMAX].bitcast(F32)
    wsl = sb("wsl", [128, NSLOT, D], BF16)
    rstd = sb("rstd", [128, L], F32)
    rcol = sb("rcol", [128, RMAX // 128 + 1], F32)
    cosr = sb("cosr", [128, RMAX], F32)
    NTMP = 8
    tmp = sb("tmp", [128, NTMP, 512], F32)
    NSQ = 5
    sq = sb("sq", [128, NSQ, 512], BF16)
    ccu = sb("ccu", [128, KC], F32)
    csm = sb("csm", [128, 2, 16], F32)
    identf = sb("identf", [128, 128], F32)
    identb = sb("identb", [128, 128], BF16)
    onesb = sb("onesb", [128, 128], BF16)
    onesf = sb("onesf", [128, 4], F32)
    epst = sb("epst", [128, 1], F32)
    trib = sb("trib", [128, 2, 128], BF16)
    gv = sb("gv", [128, 64], F32)
    ginv = sb("ginv", [128, 80], F32)
    gft = sb("gft", [128, KC], F32)
    gN = sb("gN", [128, 64], F32)
    ratio = sb("ratio", [128, 64], F32)
    cw = sb("cw", [128, 96], F32)
    esink = sb("esink", [128, 32], F32)
    mx = sb("mx", [128, 2, 20], F32)
    negM = sb("negM", [128, 2], F32)
    kbM = sb("kbM", [128, 2, 3 * NB], F32)
    esk = sb("esk", [128, 2, 4], F32)
    eskrow = sb("eskrow", [128, 2, 512], BF16)
    kb = sb("kb", [128, 3 * NB], F32)
    ss = sb("ss", [128, 8], F32)
    rso = sb("rso", [128, 2], F32)

    psum = []
    for i in range(8):
        cm = nc.psum_tensor("ps%d" % i, [128, 512], F32)
        psum.append(cm.__enter__())
        ctx.append(cm)

    rot = {"gen": 0, "tmp": 0, "sq": 0, "pt": 0, "x": 0}

    def gbank(pool):
        i = pool[rot["gen"] % len(pool)]
        rot["gen"] += 1
        return i

    def gtmp():
        i = rot["tmp"] % NTMP
        rot["tmp"] += 1
        return i

    def hkeys(s, e):
        return [("hi", kc, b) for kc in range(KC) for b in range(s // 128, (e + 127) // 128)]

    wsched = weight_schedule()
    wstate = {"issued": 0, "next": 0}

    def w_issue(upto):
        while wstate["issued"] < min(upto, len(wsched)):
            i = wstate["issued"]
            name, idx = wsched[i]
            slot = i % NSLOT
            P.dma("gpsimd", (lambda e, name=name, idx=idx, slot=slot: e.dma_start(out=wsl[:, slot, :], in_=wd[name][idx])),
                  ("w", slot), writes=[("w", slot)])
            wstate["issued"] += 1

    def w_acquire(name, idx):
        i = wstate["next"]
        assert wsched[i] == (name, idx), (i, wsched[i], name, idx)
        w_issue(i + NSLOT - 1)
        wstate["next"] += 1
        return i % NSLOT

    def w_release():
        w_issue(wstate["next"] + NSLOT - 1)

    ld = lambda out_, in_, key, wkey: P.dma("sync", (lambda e: e.dma_start(out=out_, in_=in_)), key, writes=[wkey])
    ld(gv[:], gvd[:, :], "c_gv", "gv")
    ld(cw[:], cwd[:, :], "c_cw", "cw")
    ld(identf[:], idd[:, :], "c_id", "identf")
    ld(esink[:], skd.partition_broadcast(128), "c_sk", "esink")
    P.dma("gpsimd", lambda e: e.dma_start(out=trib[:], in_=trid.rearrange("p (a b) -> p a b", a=2)), "c2", writes=["trib"])
    P.op("vector", lambda e: e.tensor_copy(out=identb[:], in_=identf[:]), reads=["identf"], writes=["identb"])
    P.op("vector", lambda e: e.memset(onesb[:], 1.0), writes=["onesb"])
    P.op("vector", lambda e: e.memset(onesf[:], 1.0), writes=["onesf"])
    P.op("vector", lambda e: e.memset(epst[:], EPS), writes=["epst"])
    P.op("vector", lambda e: e.reciprocal(out=ginv[:, 0:64], in_=gv[:]), reads=["gv"], writes=["ginv"])
    P.op("vector", lambda e: e.memset(ginv[:, 64:80], 1.0), reads=["ginv"], writes=["ginv"])
    ld(gft[:], gfd[:, :], "c_gf", "gft")
    P.op("vector", lambda e: e.tensor_copy(out=gN[:, 0:48], in_=gv[:, 16:64]), reads=["gv"], writes=["gN"])
    P.op("vector", lambda e: e.memset(gN[:, 48:64], 1.0), reads=["gN"], writes=["gN"])
    P.op("vector", lambda e: e.tensor_tensor(out=ratio[:], in0=gN[:], in1=ginv[:, 0:64], op=ALU.mult), reads=["gN", "ginv"], writes=["ratio"])

    ALLB = list(range(8))

    def update(l, m, s, e, bank, n):
        lm = l * KC + m
        t1 = gtmp()
        P.op("scalar", lambda en: en.activation(out=tmp[:, t1, 0:n], in_=psum[bank][:, 0:n], func=AF.Copy, scale=gN[:, lm:lm + 1]),
             reads=[("ps", bank), "gN"], writes=[("tmp", t1)])
        hk = [("hi", m, b) for b in range(s // 128, (e + 127) // 128)]
        if not USE_LO:
            P.op("vector", lambda en: en.scalar_tensor_tensor(out=hi[:, m, s:e], in0=hi[:, m, s:e], scalar=ratio[:, lm:lm + 1],
                                                              in1=tmp[:, t1, 0:n], op0=ALU.mult, op1=ALU.add),
                 reads=hk + [("tmp", t1), "ratio"], writes=hk)
        else:
            lk = [("lo", m, b) for b in range(s // 128, (e + 127) // 128)]
            t2 = gtmp()
            P.op("vector", lambda en: en.scalar_tensor_tensor(out=tmp[:, t2, 0:n], in0=hi[:, m, s:e], scalar=ratio[:, lm:lm + 1],
                                                              in1=tmp[:, t1, 0:n], op0=ALU.mult, op1=ALU.add),
                 reads=hk + [("tmp", t1), "ratio"], writes=[("tmp", t2)])
            P.op("vector", lambda en: en.scalar_tensor_tensor(out=tmp[:, t1, 0:n], in0=lo[:, m, s:e], scalar=ratio[:, lm:lm + 1],
                                                              in1=tmp[:, t2, 0:n], op0=ALU.mult, op1=ALU.add),
                 reads=lk + [("tmp", t2), "ratio"], writes=[("tmp", t1)])
            P.op("scalar", lambda en: en.activation(out=hi[:, m, s:e], in_=tmp[:, t1, 0:n], func=AF.Copy),
                 reads=[("tmp", t1)], writes=hk)
            P.op("vector", lambda en: en.tensor_tensor(out=lo[:, m, s:e], in0=tmp[:, t1, 0:n], in1=hi[:, m, s:e], op=ALU.subtract),
                 reads=hk + [("tmp", t1)], writes=lk)

    def rsk(s, e):
        return [("rs", b_) for b_ in range(s // 128, (e + 127) // 128)]

    def out_proj(l, wname, j, t0, t1):
        subs = subtiles(t0, t1)
        fuse = True
        rb = [5, 6, 7][:len(subs)] if fuse else []
        pool = [0, 1, 2, 3, 4] if fuse else ALLB
        pend = []

        def flush_one():
            m_, qi_, n_, si_ = pend.pop(0)
            P.mm([lambda en: en.matmul(psum[rb[si_]][:, 0:n_], onesb[:], sq[:, qi_, 0:n_], start=(m_ == 0), stop=(m_ == KC - 1))],
                 reads=[("sq", qi_), "onesb"], writes=[("ps", rb[si_])])

        for m in range(KC):
            slot = w_acquire(wname, j * KC + m)
            for si, (s, e) in enumerate(subs):
                n = e - s
                bank = gbank(pool)
                fns = []
                for kc in range(KC):
                    fns.append(lambda en, kc=kc, s=s, e=e, n=n, bank=bank, slot=slot: en.matmul(
                        psum[bank][:, 0:n], wsl[:, slot, kc * 128:(kc + 1) * 128], og[:, kc, s - t0:e - t0],
                        start=(kc == 0), stop=(kc == KC - 1)))
                ogk = [("og", kc, b_) for kc in range(KC) for b_ in range((s - t0) // 128, (e - t0 + 127) // 128)]
                P.mm(fns, reads=[("w", slot)] + ogk, writes=[("ps", bank)])
                update(l, m, s, e, bank, n)
                if fuse:
                    qi = rot["sq"] % NSQ
                    rot["sq"] += 1
                    lk = (l + 1) * KC + m
                    P.op("scalar", lambda en, m=m, qi=qi, lk=lk, n=n, s=s, e=e: en.activation(out=sq[:, qi, 0:n], in_=hi[:, m, s:e], func=AF.Square,
                                                                                scale=ginv[:, lk:lk + 1]),
                         reads=[("hi", m, b_) for b_ in range(s // 128, e // 128)] + ["ginv"], writes=[("sq", qi)])
                    pend.append((m, qi, n, si))
                    if len(pend) > 2:
                        flush_one()
            w_release()
        while pend:
            flush_one()
        if fuse:
            for si, (s, e) in enumerate(subs):
                n = e - s
                t = gtmp()
                P.op("scalar", lambda en, t=t, si=si, n=n: en.activation(out=tmp[:, t, 0:n], in_=psum[rb[si]][:, 0:n], func=AF.Sqrt,
                                                                         bias=epst[:, 0:1], scale=1.0 / D),
                     reads=[("ps", rb[si]), "epst"], writes=[("tmp", t)])
                P.op("vector", lambda en, t=t, n=n, s=s, e=e: en.reciprocal(out=rstd[:, s:e], in_=tmp[:, t, 0:n]),
                     reads=[("tmp", t)], writes=rsk(s, e))

    def rstd_phase(l, r0, r1):
        for (s, e) in subtiles(r0, r1):
            n = e - s
            bank = gbank(ALLB)
            for kc in range(KC):
                qi = rot["sq"] % NSQ
                rot["sq"] += 1
                lk = l * KC + kc
                P.op("scalar", lambda en, kc=kc, qi=qi, lk=lk, n=n, s=s, e=e: en.activation(out=sq[:, qi, 0:n], in_=hi[:, kc, s:e], func=AF.Square,
                                                                             scale=ginv[:, lk:lk + 1]),
                     reads=[("hi", kc, b_) for b_ in range(s // 128, (e + 127) // 128)] + ["ginv"], writes=[("sq", qi)])
                P.mm([lambda en, kc=kc, qi=qi, bank=bank, n=n: en.matmul(psum[bank][:, 0:n], onesb[:], sq[:, qi, 0:n],
                                                                    start=(kc == 0), stop=(kc == KC - 1))],
                     reads=[("sq", qi), "onesb"], writes=[("ps", bank)])
            t = gtmp()
            P.op("scalar", lambda en, t=t, bank=bank, n=n: en.activation(out=tmp[:, t, 0:n], in_=psum[bank][:, 0:n], func=AF.Sqrt,
                                                                    bias=epst[:, 0:1], scale=1.0 / D),
                 reads=[("ps", bank), "epst"], writes=[("tmp", t)])
            P.op("vector", lambda en, t=t, n=n, s=s, e=e: en.reciprocal(out=rstd[:, s:e], in_=tmp[:, t, 0:n]),
                 reads=[("tmp", t)], writes=rsk(s, e))

    def rope_evac(bank, n, cfn, sfn, tkeys, dst_fn, dst_keys):
        ta, tb = gtmp(), gtmp()
        P.op("vector", lambda en: en.tensor_tensor(out=tmp[:, ta, 0:n], in0=psum[bank][:, 0:n], in1=cfn(0, 128), op=ALU.mult),
             reads=[("ps", bank)] + tkeys, writes=[("tmp", ta)])
        P.op("vector", lambda en: en.tensor_tensor(out=tmp[0:64, tb, 0:n], in0=psum[bank][64:128, 0:n], in1=sfn(64, 128), op=ALU.mult),
             reads=[("ps", bank)] + tkeys, writes=[("tmp", tb)])
        P.op("vector", lambda en: en.tensor_tensor(out=tmp[64:128, tb, 0:n], in0=psum[bank][0:64, 0:n], in1=sfn(0, 64), op=ALU.mult),
             reads=[("ps", bank)] + tkeys, writes=[("tmp", tb)])
        P.op("vector", lambda en: en.tensor_tensor(out=dst_fn(), in0=tmp[:, ta, 0:n], in1=tmp[:, tb, 0:n], op=ALU.add),
             reads=[("tmp", ta), ("tmp", tb)], writes=dst_keys)

    def proj_mm(slot, s, e, bank):
        n = e - s
        fns = [lambda en, kc=kc: en.matmul(psum[bank][:, 0:n], wsl[:, slot, kc * 128:(kc + 1) * 128], hi[:, kc, s:e],
                                           start=(kc == 0), stop=(kc == KC - 1)) for kc in range(KC)]
        P.mm(fns, reads=[("w", slot)] + hkeys(s, e), writes=[("ps", bank)])

    def attn_tile(seg, l, b0, b1, has_carry, has_next, first):
        j = l // 2
        t0, t1 = b0 * 128, b1 * 128
        kv_lo = b0 if (has_carry or b0 == 0) else b0 - 1
        r1b = min(b1 + 1, NB)
        r0, r1 = kv_lo * 128, r1b * 128
        GB = [0, 1, 5]
        meta_in = (kv_lo == 0)
        nr = r1 - r0
        RS = rsk(r0, r1)
        P.dma("sync", lambda e: e.dma_start(out=cosr[:, 0:nr], in_=cosd[seg][:, r0:r1]), "cos", writes=["cosr"])
        P.dma("sync", lambda e: e.dma_start(out=sinr[:, 0:nr], in_=sind[seg][:, r0:r1]), "sin", writes=["sinr"])
        P.op("vector", lambda e: e.tensor_tensor(out=cosr[:, 0:nr], in0=cosr[:, 0:nr], in1=rstd[:, r0:r1], op=ALU.mult),
             reads=["cosr"] + RS, writes=["cosr"])
        P.op("vector", lambda e: e.tensor_tensor(out=sinr[:, 0:nr], in0=sinr[:, 0:nr], in1=rstd[:, r0:r1], op=ALU.mult),
             reads=["sinr"] + RS, writes=["sinr"])
        if first and not meta_in:
            P.dma("sync", lambda e: e.dma_start(out=csm[:, 0, :], in_=cosd[seg][:, 112:128]), "csm", writes=["csm"])
            P.dma("sync", lambda e: e.dma_start(out=csm[:, 1, :], in_=sind[seg][:, 112:128]), "csm", writes=["csm"])
            P.op("vector", lambda e: e.tensor_tensor(out=csm[:, :, :], in0=csm[:, :, :], in1=rstd[:, 112:128].unsqueeze(1).broadcast_to([128, 2, 16]), op=ALU.mult),
                 reads=["csm"] + rsk(0, 128), writes=["csm"])
        nblk = r1b - kv_lo
        bank = gbank(GB)
        fns = [lambda en, bi=bi, bank=bank: en.matmul(psum[bank][:, bi:bi + 1], rstd[0:1, r0 + bi * 128:r0 + (bi + 1) * 128], onesf[0:1, 0:1],
                                                      start=True, stop=True) for bi in range(nblk)]
        rr = RS + ["onesf"]
        if first:
            fns.append(lambda en, bank=bank: en.matmul(psum[bank][0:16, nblk:nblk + 1], rstd[0:1, 112:128], onesf[0:1, 0:1], start=True, stop=True))
            rr = rr + rsk(0, 128)
        P.mm(fns, reads=rr, writes=[("ps", bank)])
        ncol = nblk + 1
        P.op("vector", lambda e, bank=bank: e.tensor_copy(out=rcol[:, 0:ncol], in_=psum[bank][:, 0:ncol]),
             reads=[("ps", bank)], writes=["rcol"])
        cfn = lambda c0, n: (lambda p0, p1: cosr[p0:p1, c0:c0 + n])
        sfn = lambda c0, n: (lambda p0, p1: sinr[p0:p1, c0:c0 + n])

        def kvq(g):
            base = (j * 4 + g) * 10
            gb = g % 2
            slot = w_acquire("w_ai", base + 4)
            for (s, e) in subtiles(r0, r1):
                n = e - s
                bank = gbank(GB)
                proj_mm(slot, s, e, bank)
                rope_evac(bank, n, cfn(s - r0, n), sfn(s - r0, n), ["cosr", "sinr"], (lambda s=s, e=e: kT[:, gb, s - r0:e - r0]),
                          [("kT", gb, b_) for b_ in range((s - r0) // 128, (e - r0) // 128)])
                yield
            if first and meta_in:
                P.op("scalar", lambda en: en.activation(out=kmeta[:, g, :], in_=kT[:, gb, 112:128], func=AF.Copy),
                     reads=[("kT", gb, 0)], writes=[("kmeta", g)])
            elif first:
                bank = gbank(GB)
                proj_mm(slot, 112, 128, bank)
                rope_evac(bank, 16, (lambda p0, p1: csm[p0:p1, 0, :]), (lambda p0, p1: csm[p0:p1, 1, :]), ["csm"],
                          (lambda: kmeta[:, g, :]), [("kmeta", g)])
                yield
            w_release()
            slot = w_acquire("w_ai", base + 5)
            for bi in range(nblk):
                s = r0 + bi * 128
                if bi % 4 == 0:
                    bank = gbank(GB)
                c0 = (bi % 4) * 128
                fns = [lambda en, kc=kc, s=s, bank=bank, slot=slot, c0=c0: en.matmul(
                    psum[bank][:, c0:c0 + 128], hi[:, kc, s:s + 128], wsl[:, slot, kc * 128:(kc + 1) * 128], start=(kc == 0), stop=(kc == KC - 1))
                    for kc in range(KC)]
                P.mm(fns, reads=[("w", slot)] + hkeys(s, s + 128), writes=[("ps", bank)])
                P.op("scalar", lambda en, bi=bi, bank=bank, c0=c0: en.activation(out=vv[:, gb, bi, :], in_=psum[bank][:, c0:c0 + 128], func=AF.Copy,
                                                                                 scale=rcol[:, bi:bi + 1]),
                     reads=[("ps", bank), "rcol"], writes=[("v", gb, bi)])
                yield
            if first:
                bank = gbank(GB)
                fns = [lambda en, kc=kc, bank=bank, slot=slot: en.matmul(
                    psum[bank][0:16, 0:128], hi[:, kc, 112:128], wsl[:, slot, kc * 128:(kc + 1) * 128], start=(kc == 0), stop=(kc == KC - 1))
                    for kc in range(KC)]
                P.mm(fns, reads=[("w", slot)] + hkeys(0, 128), writes=[("ps", bank)])
                P.op("scalar", lambda en, bank=bank: en.activation(out=vmeta[0:16, g, :], in_=psum[bank][0:16, 0:128], func=AF.Copy,
                                                                   scale=rcol[0:16, nblk:nblk + 1]),
                     reads=[("ps", bank), "rcol"], writes=[("vmeta", g)])
                yield
            w_release()
            for h in range(4):
                slot = w_acquire("w_ai", base + h)
                for (s, e) in subtiles(t0, t1):
                    n = e - s
                    bank = gbank(GB)
                    proj_mm(slot, s, e, bank)
                    rope_evac(bank, n, cfn(s - r0, n), sfn(s - r0, n), ["cosr", "sinr"],
                              (lambda s=s, e=e, h=h: og[:, 4 * g + h, s - t0:e - t0]),
                              [("og", 4 * g + h, b_) for b_ in range((s - t0) // 128, (e - t0) // 128)])
                    yield
                w_release()

        def stab(g):
            gb = g % 2
            srcs = []
            for h in range(4):
                for (s_, e_) in subtiles(0, t1 - t0):
                    srcs.append((lambda h=h, s_=s_, e_=e_: og[:, 4 * g + h, s_:e_], e_ - s_,
                                 [("og", 4 * g + h, b_) for b_ in range(s_ // 128, e_ // 128)], True))
            for (s_, e_) in subtiles(0, nr):
                srcs.append((lambda s_=s_, e_=e_: kT[:, gb, s_:e_], e_ - s_, [("kT", gb, b_) for b_ in range(s_ // 128, e_ // 128)], False))
            if has_carry:
                srcs.append((lambda: ck[:, g, :], 128, [("ck", g)], False))
            srcs.append((lambda: kmeta[:, g, :], 16, [("kmeta", g)], False))
            nq = sum(1 for x in srcs if x[3])
            nk = len(srcs) - nq
            assert nq <= 12 and nk <= 6
            iq, ik = 0, 0
            pend = []

            def flush():
                qi_, n_, col_ = pend.pop(0)
                bank = gbank(GB)
                P.mm([lambda en: en.matmul(psum[bank][:, 0:n_], onesb[:], sq[:, qi_, 0:n_], start=True, stop=True)],
                     reads=[("sq", qi_), "onesb"], writes=[("ps", bank)])
                P.op("vector", lambda en: en.reduce_max(out=mx[:, gb, col_:col_ + 1], in_=psum[bank][:, 0:n_], axis=AX.X),
                     reads=[("ps", bank)], writes=[("mx", gb, col_)])

            for (apf, n, rk, isq) in srcs:
                qi = rot["sq"] % NSQ
                rot["sq"] += 1
                col = iq if isq else 12 + ik
                if isq:
                    iq += 1
                else:
                    ik += 1
                P.op("scalar", lambda en, apf=apf, qi=qi, n=n: en.activation(out=sq[:, qi, 0:n], in_=apf(), func=AF.Square), reads=rk, writes=[("sq", qi)])
                pend.append((qi, n, col))
                if len(pend) > 2:
                    flush()
                yield
            while pend:
                flush()
            yield
            P.op("vector", lambda en: en.reduce_max(out=mx[:, gb, 18:19], in_=mx[:, gb, 0:nq], axis=AX.X),
                 reads=[("mx", gb, c_) for c_ in range(nq)], writes=[("mx", gb, 18)])
            P.op("vector", lambda en: en.reduce_max(out=mx[:, gb, 19:20], in_=mx[:, gb, 12:12 + nk], axis=AX.X),
                 reads=[("mx", gb, 12 + c_) for c_ in range(nk)], writes=[("mx", gb, 19)])
            P.op("vector", lambda en: en.tensor_tensor(out=negM[:, gb:gb + 1], in0=mx[:, gb, 18:19], in1=mx[:, gb, 19:20], op=ALU.add),
                 reads=[("mx", gb, 18), ("mx", gb, 19)], writes=[("stab", gb)])
            P.op("vector", lambda en: en.tensor_scalar(out=negM[:, gb:gb + 1], in0=negM[:, gb:gb + 1], scalar1=-0.5 * SCALE, scalar2=None, op0=ALU.mult),
                 reads=[("stab", gb)], writes=[("stab", gb)])
            P.op("vector", lambda en: en.tensor_scalar(out=kbM[:, gb, :], in0=kb[:, :], scalar1=negM[:, gb:gb + 1], scalar2=None, op0=ALU.add),
                 reads=[("stab", gb), "kb"], writes=[("stabk", gb)])
            ec = j * 16 + 4 * g
            P.op("scalar", lambda en, ec=ec: en.activation(out=esk[:, gb, :], in_=esink[:, ec:ec + 4], func=AF.Exp, bias=negM[:, gb:gb + 1], scale=1.0),
                 reads=[("stab", gb), "esink"], writes=[("stabe", gb)])
            P.op("vector", lambda en: en.tensor_copy(out=eskrow[0:1, gb, :].rearrange("p (a b) -> p a b", a=4),
                                                     in_=esk[0:1, gb, :].unsqueeze(2).broadcast_to([1, 4, 128])),
                 reads=[("stabe", gb)], writes=[("stabr", gb)])
            yield

        def kvq_s(g):
            for _ in kvq(g):
                yield
            for _ in stab(g):
                yield

        def zproj(g):
            base = (j * 4 + g) * 10
            for h in range(4):
                slot = w_acquire("w_ai", base + 6 + h)
                for (s, e) in subtiles(t0, t1):
                    n = e - s
                    bank = gbank(GB)
                    proj_mm(slot, s, e, bank)
                    ta, tb = gtmp(), gtmp()
                    P.op("vector", lambda en, ta=ta, bank=bank, n=n, s=s: en.tensor_tensor(out=tmp[:, ta, 0:n], in0=psum[bank][:, 0:n],
                                                                                            in1=rstd[:, s:s + n], op=ALU.mult),
                         reads=[("ps", bank)] + rsk(s, e), writes=[("tmp", ta)])
                    P.op("scalar", lambda en, ta=ta, tb=tb, n=n: en.activation(out=tmp[:, tb, 0:n], in_=tmp[:, ta, 0:n], func=AF.Tanh, scale=0.5),
                         reads=[("tmp", ta)], writes=[("tmp", tb)])
                    P.op("vector", lambda en, ta=ta, tb=tb, n=n, s=s, e=e, h=h: en.scalar_tensor_tensor(
                        out=sz[:, h, s - t0:e - t0], in0=tmp[:, tb, 0:n], scalar=1.0, in1=tmp[:, ta, 0:n], op0=ALU.add, op1=ALU.mult),
                         reads=[("tmp", ta), ("tmp", tb)], writes=[("sz", h, b_) for b_ in range((s - t0) // 128, (e - t0) // 128)])
                w_release()

        def core(g):
            gb = g % 2
            for b in range(b0, b1):
                bl = b - b0
                kl = b - kv_lo
                qs, qe = bl * 128, (bl + 1) * 128
                pset = rot["pt"] % 2
                rot["pt"] += 1
                qkeys = [("og", 4 * g + h, bl) for h in range(4)]
                tiles = []
                if b >= 1:
                    if kl >= 1:
                        tiles.append((2, (lambda kl=kl: kT[:, gb, (kl - 1) * 128:kl * 128]), (lambda kl=kl: vv[:, gb, kl - 1, :]),
                                      [("kT", gb, kl - 1)], [("v", gb, kl - 1)], 0 * NB + b, 0))
                    else:
                        assert has_carry
                        tiles.append((2, (lambda: ck[:, g, :]), (lambda: cv[:, g, :]), [("ck", g)], [("cv", g)], 0 * NB + b, 0))
                tiles.append((3, (lambda kl=kl: kT[:, gb, kl * 128:(kl + 1) * 128]), (lambda kl=kl: vv[:, gb, kl, :]),
                              [("kT", gb, kl)], [("v", gb, kl)], 1 * NB + b, None))
                if b <= NB - 2:
                    tiles.append((4, (lambda kl=kl: kT[:, gb, (kl + 1) * 128:(kl + 2) * 128]), (lambda kl=kl: vv[:, gb, kl + 1, :]),
                                  [("kT", gb, kl + 1)], [("v", gb, kl + 1)], 2 * NB + b, 1))
                qap = lambda qs=qs, qe=qe: og[:, 4 * g:4 * g + 4, qs:qe]
                for ti, (bank, kfn, vfn, kk, vk, kbc, tri) in enumerate(tiles):
                    fns = [lambda en, bank=bank, kfn=kfn, tri=tri, qap=qap: en.matmul(psum[bank][:, :], kfn(), qap(), start=True, stop=(tri is None))]
                    rd = kk + qkeys
                    if tri is not None:
                        fns.append(lambda en, bank=bank, tri=tri: en.matmul(psum[bank][:, :], identb[:], trib[:, tri, :].unsqueeze(1).broadcast_to([128, 4, 128]),
                                                                            start=False, stop=True))
                        rd = rd + ["identb", "trib"]
                    P.mm(fns, reads=rd, writes=[("ps", bank)])
                    P.op("scalar", lambda en, bank=bank, ti=ti, kbc=kbc, pset=pset: en.activation(
                        out=PT[:, pset, ti, :], in_=psum[bank][:, :], func=AF.Exp, bias=kbM[:, gb, kbc:kbc + 1], scale=SCALE),
                         reads=[("ps", bank), ("stabk", gb)], writes=[("PT", pset, ti)])
                P.mm([lambda en, qap=qap: en.matmul(psum[7][0:16, :], kmeta[:, g, :], qap(), start=True, stop=True)],
                     reads=[("kmeta", g)] + qkeys, writes=[("ps", 7)])
                P.op("scalar", lambda en, pset=pset: en.activation(out=PT[0:16, pset, 3, :], in_=psum[7][0:16, :], func=AF.Exp, bias=negM[0:16, gb:gb + 1], scale=SCALE),
                     reads=[("ps", 7), ("stab", gb)], writes=[("PT", pset, 3)])
                yield
                nt = len(tiles)
                fns, rd = [], []
                for ti, (bank, kfn, vfn, kk, vk, kbc, tri) in enumerate(tiles):
                    fns.append(lambda en, ti=ti, vfn=vfn, pset=pset: en.matmul(psum[6][:, :], vfn(), PT[:, pset, ti, :], start=(ti == 0), stop=False))
                    rd += vk + [("PT", pset, ti)]
                fns.append(lambda en, pset=pset: en.matmul(psum[6][:, :], vmeta[0:16, g, :], PT[0:16, pset, 3, :], start=False, stop=True))
                P.mm(fns, reads=rd + [("vmeta", g), ("PT", pset, 3)], writes=[("ps", 6)])
                fns = []
                for ti in range(nt):
                    fns.append(lambda en, ti=ti, pset=pset: en.matmul(psum[7][:, :], onesb[:], PT[:, pset, ti, :], start=(ti == 0), stop=False))
                fns.append(lambda en, pset=pset: en.matmul(psum[7][:, :], onesb[0:16, :], PT[0:16, pset, 3, :], start=False, stop=False))
                fns.append(lambda en: en.matmul(psum[7][:, :], onesb[0:1, :], eskrow[0:1, gb, :], start=False, stop=True))
                P.mm(fns, reads=[("PT", pset, ti) for ti in range(nt)] + [("PT", pset, 3), "onesb", ("stabr", gb)], writes=[("ps", 7)])
                ta, tb = gtmp(), gtmp()
                ec = j * 16 + 4 * g
                P.op("vector", lambda en, ta=ta: en.reciprocal(out=tmp[:, ta, :], in_=psum[7][:, :]), reads=[("ps", 7)], writes=[("tmp", ta)])
                P.op("vector", lambda en, ta=ta, tb=tb: en.tensor_tensor(out=tmp[:, tb, :], in0=psum[6][:, :], in1=tmp[:, ta, :], op=ALU.mult),
                     reads=[("ps", 6), ("tmp", ta)], writes=[("tmp", tb)])
                P.op("vector", lambda en, tb=tb, qs=qs, qe=qe: en.scalar_tensor_tensor(
                    out=og[:, 4 * g:4 * g + 4, qs:qe], in0=tmp[:, tb, :].rearrange("p (a b) -> p a b", a=4), scalar=0.5,
                    in1=sz[:, :, qs:qe], op0=ALU.mult, op1=ALU.mult),
                     reads=[("tmp", tb)] + [("sz", h, bl) for h in range(4)], writes=qkeys)
                yield
            if has_next:
                lb = b1 - 1 - kv_lo
                P.op("scalar", lambda en, lb=lb: en.activation(out=ck[:, g, :], in_=kT[:, gb, lb * 128:(lb + 1) * 128], func=AF.Copy),
                     reads=[("kT", gb, lb)], writes=[("ck", g)])
                P.op("scalar", lambda en, lb=lb: en.activation(out=cv[:, g, :], in_=vv[:, gb, lb, :], func=AF.Copy),
                     reads=[("v", gb, lb)], writes=[("cv", g)])

        def core_pipe(g):
            gb = g % 2
            sets = [[2, 3, 4], [0, 1, 5]]
            st = {}

            def scores(b):
                bl, kl = b - b0, b - kv_lo
                qs, qe = bl * 128, (bl + 1) * 128
                bs = sets[bl % 2]
                pset = bl % 2
                qkeys = [("og", 4 * g + h, bl) for h in range(4)]
                tiles = []
                if b >= 1:
                    if kl >= 1:
                        tiles.append((bs[0], (lambda kl=kl: kT[:, gb, (kl - 1) * 128:kl * 128]), (lambda kl=kl: vv[:, gb, kl - 1, :]),
                                      [("kT", gb, kl - 1)], [("v", gb, kl - 1)], 0 * NB + b, 0))
                    else:
                        assert has_carry
                        tiles.append((bs[0], (lambda: ck[:, g, :]), (lambda: cv[:, g, :]), [("ck", g)], [("cv", g)], 0 * NB + b, 0))
                tiles.append((bs[1], (lambda kl=kl: kT[:, gb, kl * 128:(kl + 1) * 128]), (lambda kl=kl: vv[:, gb, kl, :]),
                              [("kT", gb, kl)], [("v", gb, kl)], 1 * NB + b, None))
                if b <= NB - 2:
                    tiles.append((bs[2], (lambda kl=kl: kT[:, gb, (kl + 1) * 128:(kl + 2) * 128]), (lambda kl=kl: vv[:, gb, kl + 1, :]),
                                  [("kT", gb, kl + 1)], [("v", gb, kl + 1)], 2 * NB + b, 1))
                qap = lambda qs=qs, qe=qe: og[:, 4 * g:4 * g + 4, qs:qe]
                for ti, (bank, kfn, vfn, kk, vk, kbc, tri) in enumerate(tiles):
                    fns = [lambda en, bank=bank, kfn=kfn, tri=tri, qap=qap: en.matmul(psum[bank][:, :], kfn(), qap(), start=True, stop=(tri is None))]
                    rd = kk + qkeys
                    if tri is not None:
                        fns.append(lambda en, bank=bank, tri=tri: en.matmul(psum[bank][:, :], identb[:], trib[:, tri, :].unsqueeze(1).broadcast_to([128, 4, 128]),
                                                                            start=False, stop=True))
                        rd = rd + ["identb", "trib"]
                    P.mm(fns, reads=rd, writes=[("ps", bank)])
                    P.op("scalar", lambda en, bank=bank, ti=ti, kbc=kbc, pset=pset: en.activation(
                        out=PT[:, pset, ti, :], in_=psum[bank][:, :], func=AF.Exp, bias=kbM[:, gb, kbc:kbc + 1], scale=SCALE),
                         reads=[("ps", bank), ("stabk", gb)], writes=[("PT", pset, ti)])
                st[b] = (tiles, qap, qkeys, pset, qs, qe, bl)

            def meta_scores(b):
                tiles, qap, qkeys, pset, qs, qe, bl = st[b]
                P.mm([lambda en, qap=qap: en.matmul(psum[7][0:16, :], kmeta[:, g, :], qap(), start=True, stop=True)],
                     reads=[("kmeta", g)] + qkeys, writes=[("ps", 7)])
                P.op("scalar", lambda en, pset=pset: en.activation(out=PT[0:16, pset, 3, :], in_=psum[7][0:16, :], func=AF.Exp, bias=negM[0:16, gb:gb + 1], scale=SCALE),
                     reads=[("ps", 7), ("stab", gb)], writes=[("PT", pset, 3)])

            def finish(b):
                tiles, qap, qkeys, pset, qs, qe, bl = st.pop(b)
                nt = len(tiles)
                fns, rd = [], []
                for ti, (bank, kfn, vfn, kk, vk, kbc, tri) in enumerate(tiles):
                    fns.append(lambda en, ti=ti, vfn=vfn, pset=pset: en.matmul(psum[6][:, :], vfn(), PT[:, pset, ti, :], start=(ti == 0), stop=False))
                    rd += vk + [("PT", pset, ti)]
                fns.append(lambda en, pset=pset: en.matmul(psum[6][:, :], vmeta[0:16, g, :], PT[0:16, pset, 3, :], start=False, stop=True))
                P.mm(fns, reads=rd + [("vmeta", g), ("PT", pset, 3)], writes=[("ps", 6)])
                fns = []
                for ti in range(nt):
                    fns.append(lambda en, ti=ti, pset=pset: en.matmul(psum[7][:, :], onesb[:], PT[:, pset, ti, :], start=(ti == 0), stop=False))
                fns.append(lambda en, pset=pset: en.matmul(psum[7][:, :], onesb[0:16, :], PT[0:16, pset, 3, :], start=False, stop=False))
                fns.append(lambda en: en.matmul(psum[7][:, :], onesb[0:1, :], eskrow[0:1, gb, :], start=False, stop=True))
                P.mm(fns, reads=[("PT", pset, ti) for ti in range(nt)] + [("PT", pset, 3), "onesb", ("stabr", gb)], writes=[("ps", 7)])
                ta, tb = gtmp(), gtmp()
                ec = j * 16 + 4 * g
                P.op("vector", lambda en, ta=ta: en.reciprocal(out=tmp[:, ta, :], in_=psum[7][:, :]), reads=[("ps", 7)], writes=[("tmp", ta)])
                P.op("vector", lambda en, ta=ta, tb=tb: en.tensor_tensor(out=tmp[:, tb, :], in0=psum[6][:, :], in1=tmp[:, ta, :], op=ALU.mult),
                     reads=[("ps", 6), ("tmp", ta)], writes=[("tmp", tb)])
                P.op("vector", lambda en, tb=tb, qs=qs, qe=qe: en.scalar_tensor_tensor(
                    out=og[:, 4 * g:4 * g + 4, qs:qe], in0=tmp[:, tb, :].rearrange("p (a b) -> p a b", a=4), scalar=0.5,
                    in1=sz[:, :, qs:qe], op0=ALU.mult, op1=ALU.mult),
                     reads=[("tmp", tb)] + [("sz", h, bl) for h in range(4)], writes=qkeys)

            scores(b0)
            meta_scores(b0)
            for b in range(b0, b1):
                if b + 1 < b1:
                    scores(b + 1)
                finish(b)
                if b + 1 < b1:
                    meta_scores(b + 1)
            if has_next:
                lb = b1 - 1 - kv_lo
                P.op("scalar", lambda en, lb=lb: en.activation(out=ck[:, g, :], in_=kT[:, gb, lb * 128:(lb + 1) * 128], func=AF.Copy),
                     reads=[("kT", gb, lb)], writes=[("ck", g)])
                P.op("scalar", lambda en, lb=lb: en.activation(out=cv[:, g, :], in_=vv[:, gb, lb, :], func=AF.Copy),
                     reads=[("v", gb, lb)], writes=[("cv", g)])

        def run(gen):
            for _ in gen:
                pass

        def interleave(main, filler, n_main, n_fill):
            done_f = 0
            fill_alive = True
            for i, _ in enumerate(main):
                want = ((i + 1) * n_fill + n_main - 1) // n_main
                while fill_alive and done_f < want:
                    try:
                        next(filler)
                        done_f += 1
                    except StopIteration:
                        fill_alive = False
            if fill_alive:
                run(filler)

        n_core = 2 * (b1 - b0)
        n_kvq = len(subtiles(r0, r1)) + nblk + 4 * len(subtiles(t0, t1)) + (2 if first else 0) \
            + 4 * len(subtiles(t0, t1)) + len(subtiles(0, nr)) + 3
        run(kvq_s(0))
        zproj(0)
        for g in range(4):
            if g < 3:
                interleave(core(g), kvq_s(g + 1), n_core, n_kvq)
                zproj(g + 1)
            else:
                core_pipe(g)
        out_proj(l, "w_ao", j, t0, t1)

    def conv_tile(seg, l, b0, b1, has_carry, has_next):
        j = l // 2
        t0, t1 = b0 * 128, b1 * 128
        T = t1 - t0
        left_ext = (not has_carry) and b0 > 0
        r0 = (b0 - 1) * 128 if left_ext else t0
        r1 = min(b1 + 1, NB) * 128
        xs_ = t0 - 1 if left_ext else t0
        xe = min(t1 + 1, L)
        nr = r1 - r0
        P.dma("sync", lambda e: e.dma_start(out=cosr[:, 0:nr], in_=okd[seg, r0:r1].partition_broadcast(128)), "cos", writes=["cosr"])
        P.op("vector", lambda e: e.tensor_tensor(out=cosr[:, 0:nr], in0=cosr[:, 0:nr], in1=rstd[:, r0:r1], op=ALU.mult),
             reads=["cosr"] + rsk(r0, r1), writes=["cosr"])
        P.op("vector", lambda e: e.tensor_tensor(out=cosr[:, 0:nr], in0=cosr[:, 0:nr], in1=rstd[:, r0:r1], op=ALU.mult),
             reads=["cosr"] + rsk(r0, r1), writes=["cosr"])
        xsubs = subtiles(xs_, xe)
        for f in range(KC):
            base = (j * KC + f) * 4
            cb = f % 2
            if not left_ext:
                if b0 == 0:
                    P.op("vector", lambda en, cb=cb: en.memset(cu[:, cb, 0:1], 0.0), writes=[("cu", cb, "l")])
                else:
                    P.op("vector", lambda en, cb=cb, f=f: en.tensor_copy(out=cu[:, cb, 0:1], in_=ccu[:, f:f + 1]), reads=[("ccu", f)], writes=[("cu", cb, "l")])
            if xe == t1:
                P.op("vector", lambda en, cb=cb: en.memset(cu[:, cb, T + 1:T + 2], 0.0), writes=[("cu", cb, "r")])
            slot_c = w_acquire("w_ci", base + 1)
            csb = []
            for (s, e) in xsubs:
                n = e - s
                bank = gbank(ALLB)
                proj_mm(slot_c, s, e, bank)
                ta = gtmp()
                P.op("scalar", lambda en, ta=ta, bank=bank, n=n: en.activation(out=tmp[:, ta, 0:n], in_=psum[bank][:, 0:n], func=AF.Copy),
                     reads=[("ps", bank)], writes=[("tmp", ta)])
                P.op("vector", lambda en, ta=ta, n=n, s=s: en.tensor_tensor(out=tmp[:, ta, 0:n], in0=tmp[:, ta, 0:n], in1=cosr[:, s - r0:s - r0 + n], op=ALU.mult),
                     reads=[("tmp", ta), "cosr"], writes=[("tmp", ta)])
                csb.append(ta)
            w_release()
            assert len(csb) <= NTMP - 2
            slot_u = w_acquire("w_ci", base + 2)
            for si, (s, e) in enumerate(xsubs):
                n = e - s
                bank = gbank(ALLB)
                proj_mm(slot_u, s, e, bank)
                ta = csb[si]
                P.op("vector", lambda en, ta=ta, bank=bank, n=n, s=s, cb=cb: en.tensor_tensor(
                    out=cu[:, cb, 1 + s - t0:1 + s - t0 + n], in0=psum[bank][:, 0:n], in1=tmp[:, ta, 0:n], op=ALU.mult),
                     reads=[("ps", bank), ("tmp", ta)], writes=[("cu", cb, si)])
            w_release()
            cukeys = [("cu", cb, "l"), ("cu", cb, "r")] + [("cu", cb, si) for si in range(len(xsubs))]
            slot_b = w_acquire("w_ci", base + 0)
            for si, (s, e) in enumerate(subtiles(t0, t1)):
                n = e - s
                bank = gbank(ALLB)
                proj_mm(slot_b, s, e, bank)
                P.op("vector", lambda en, bank=bank, n=n, s=s, cb=cb: en.tensor_tensor(
                    out=gate[:, cb, s - t0:s - t0 + n], in0=psum[bank][:, 0:n], in1=rstd[:, s:s + n], op=ALU.mult),
                     reads=[("ps", bank)] + rsk(s, e), writes=[("gate", cb, si)])
            w_release()
            slot_z = w_acquire("w_ci", base + 3)
            for si, (s, e) in enumerate(subtiles(t0, t1)):
                n = e - s
                bank = gbank(ALLB)
                proj_mm(slot_z, s, e, bank)
                ta, tb = gtmp(), gtmp()
                P.op("vector", lambda en, ta=ta, bank=bank, n=n, s=s: en.tensor_tensor(out=tmp[:, ta, 0:n], in0=psum[bank][:, 0:n],
                                                                                        in1=rstd[:, s:s + n], op=ALU.mult),
                     reads=[("ps", bank)] + rsk(s, e), writes=[("tmp", ta)])
                P.op("scalar", lambda en, ta=ta, tb=tb, n=n: en.activation(out=tmp[:, tb, 0:n], in_=tmp[:, ta, 0:n], func=AF.Tanh, scale=0.5),
                     reads=[("tmp", ta)], writes=[("tmp", tb)])
                P.op("vector", lambda en, ta=ta, tb=tb, n=n: en.scalar_tensor_tensor(
                    out=tmp[:, tb, 0:n], in0=tmp[:, tb, 0:n], scalar=1.0, in1=tmp[:, ta, 0:n], op0=ALU.add, op1=ALU.mult),
                     reads=[("tmp", ta), ("tmp", tb)], writes=[("tmp", tb)])
                P.op("vector", lambda en, tb=tb, n=n, s=s, cb=cb: en.tensor_tensor(
                    out=gate[:, cb, s - t0:s - t0 + n], in0=gate[:, cb, s - t0:s - t0 + n], in1=tmp[:, tb, 0:n], op=ALU.mult),
                     reads=[("gate", cb, si), ("tmp", tb)], writes=[("gate", cb, si)])
            w_release()
            wc = (j * 3) * KC + f
            P.op("vector", lambda en, cb=cb, wc=wc: en.tensor_scalar(out=ybuf[:, 0:T], in0=cu[:, cb, 0:T], scalar1=cw[:, wc:wc + 1], scalar2=None, op0=ALU.mult),
                 reads=cukeys + ["cw"], writes=["ybuf"])
            P.op("vector", lambda en, cb=cb, wc=wc: en.scalar_tensor_tensor(out=ybuf[:, 0:T], in0=cu[:, cb, 1:T + 1], scalar=cw[:, wc + KC:wc + KC + 1],
                                                                            in1=ybuf[:, 0:T], op0=ALU.mult, op1=ALU.add),
                 reads=cukeys + ["cw", "ybuf"], writes=["ybuf"])
            P.op("vector", lambda en, cb=cb, wc=wc: en.scalar_tensor_tensor(out=ybuf[:, 0:T], in0=cu[:, cb, 2:T + 2], scalar=cw[:, wc + 2 * KC:wc + 2 * KC + 1],
                                                                            in1=ybuf[:, 0:T], op0=ALU.mult, op1=ALU.add),
                 reads=cukeys + ["cw", "ybuf"], writes=["ybuf"])
            nsub = len(subtiles(t0, t1))
            P.op("vector", lambda en, cb=cb, f=f: en.scalar_tensor_tensor(out=og[:, f, 0:T], in0=ybuf[:, 0:T], scalar=0.5, in1=gate[:, cb, 0:T],
                                                                          op0=ALU.mult, op1=ALU.mult),
                 reads=["ybuf"] + [("gate", cb, si) for si in range(nsub)], writes=[("og", f, b_) for b_ in range(b1 - b0)])
            if has_next:
                P.op("vector", lambda en, cb=cb, f=f: en.tensor_copy(out=ccu[:, f:f + 1], in_=cu[:, cb, T:T + 1]), reads=cukeys, writes=[("ccu", f)])
        out_proj(l, "w_co", j, t0, t1)

    def load_segment(seg):
        P.dma("sync", lambda e: e.dma_start(out=kb[:], in_=kbd[seg]), "kb", writes=["kb"])
        for blk in range(NB):
            xb = blk % NXIN
            P.dma("sync", lambda e, blk=blk, xb=xb: e.dma_start(out=xin[:, xb, :], in_=xs[seg, blk]), ("x", xb), writes=[("xin", xb)])
            hk = [("hi", kc, blk) for kc in range(KC)]
            assert not USE_LO
            P.op("vector", lambda en, blk=blk, xb=xb: en.tensor_tensor(
                out=hi[:, :, blk * 128:(blk + 1) * 128], in0=xin[:, xb, :].rearrange("p (a b) -> p a b", a=KC),
                in1=gv[:, 0:KC].unsqueeze(2).broadcast_to([128, KC, 128]), op=ALU.mult),
                 reads=[("xin", xb), "gv"], writes=hk)
            bank = gbank(ALLB)
            sqk = [("sq", q_) for q_ in range(4)]
            P.op("scalar", lambda en, xb=xb: en.activation(out=sq[:, 0:4, :], in_=xin[:, xb, :].rearrange("p (a b) -> p a b", a=4), func=AF.Square),
                 reads=[("xin", xb)], writes=sqk)
            fns = [lambda en, kc=kc, bank=bank: en.matmul(psum[bank][:, 0:128], onesb[:], sq[:, kc // 4, (kc % 4) * 128:(kc % 4 + 1) * 128],
                                                         start=(kc == 0), stop=(kc == KC - 1)) for kc in range(KC)]
            P.mm(fns, reads=sqk + ["onesb"], writes=[("ps", bank)])
            t = gtmp()
            P.op("scalar", lambda en, t=t, bank=bank: en.activation(out=tmp[:, t, 0:128], in_=psum[bank][:, 0:128], func=AF.Sqrt, bias=epst[:, 0:1], scale=1.0 / D),
                 reads=[("ps", bank), "epst"], writes=[("tmp", t)])
            P.op("vector", lambda en, t=t, blk=blk: en.reciprocal(out=rstd[:, blk * 128:(blk + 1) * 128], in_=tmp[:, t, 0:128]),
                 reads=[("tmp", t)], writes=[("rs", blk)])

    def store_segment(seg):
        for oi, blk in enumerate(OUT_BLOCKS[seg]):
            hk_all = [("hi", kc, blk) for kc in range(KC)]
            P.op("vector", lambda en, blk=blk: en.tensor_tensor(
                out=hf, in0=hi[:, :, blk * 128:(blk + 1) * 128], in1=rstd[:, blk * 128:(blk + 1) * 128].unsqueeze(1).broadcast_to([128, KC, 128]), op=ALU.mult),
                 reads=hk_all + rsk(blk * 128, (blk + 1) * 128), writes=[("xin", 2)])
            xb = rot["x"] % 2
            rot["x"] += 1
            P.op("vector", lambda en, xb=xb: en.tensor_tensor(
                out=xin[:, xb, :].rearrange("p (a b) -> p a b", a=KC), in0=hf, in1=gft[:, :].unsqueeze(2).broadcast_to([128, KC, 128]), op=ALU.mult),
                 reads=[("xin", 2), "gft"], writes=[("xin", xb)])
            P.dma("sync", lambda e, oi=oi, xb=xb: e.dma_start(out=outd[seg][oi], in_=xin[:, xb, :]), ("o", xb), reads=[("xin", xb)])

    w_issue(NSLOT - 1)
    for seg in range(2):
        load_segment(seg)
        for l in range(N_LAYERS):
            P.barrier()
            tl = SEG_TILES[seg][l]
            for ti_, (b0, b1) in enumerate(tl):
                has_carry = ti_ > 0 and tl[ti_ - 1][1] == b0
                has_next = ti_ + 1 < len(tl) and tl[ti_ + 1][0] == b1
                if l % 2 == 0:
                    attn_tile(seg, l, b0, b1, has_carry, has_next, ti_ == 0)
                else:
                    conv_tile(seg, l, b0, b1, has_carry, has_next)
        P.barrier()
        store_segment(seg)
    assert wstate["next"] == len(wsched)
    P.wait_all("sync", P.all_events())
    P.emit()
    P.close()
    for cm in reversed(ctx):
        cm.__exit__(None, None, None)
    return nc


def _segment_meta(core):
    res = []
    pos = np.arange(L, dtype=np.int64) - 112
    res.append((pos, pos >= 0, list(range(NB)), "s"))
    real = [0, 1, 2] + [8 * core - 2 + i for i in range(14)]
    posp = np.zeros(L, dtype=np.int64)
    okp = np.zeros(L, dtype=bool)
    for vb, r in enumerate(real):
        sl = slice(vb * 128, (vb + 1) * 128)
        if 0 <= r <= 64:
            p = r * 128 + np.arange(128) - 112
            posp[sl] = p
            okp[sl] = p >= 0
        else:
            posp[sl] = -10 ** 7 - vb * 1000 - np.arange(128)
            okp[sl] = False
    res.append((posp, okp, real, "p"))
    return res


def _tables(pos, ok, lead_blocks):
    half = HEAD // 2
    inv_freq = (np.float32(10000.0) ** (-np.arange(0, half, dtype=np.float32) * np.float32(2.0 / HEAD))).astype(np.float32)
    posf = np.where(ok, pos, 0).astype(np.float32)
    ang = posf[:, None] * inv_freq[None, :]
    c = np.cos(ang).astype(np.float32).T
    s = np.sin(ang).astype(np.float32).T
    cosT = np.concatenate([c, c], 0)
    sinT = np.concatenate([s, -s], 0)
    kbt = np.full((128, 3 * NB), NEGB, dtype=np.float32)
    ii = np.arange(128)
    tri = {0: (ii[None, :] <= ii[:, None]), 1: np.ones((128, 128), bool), 2: (ii[:, None] <= ii[None, :])}
    for b in range(NB):
        pq = pos[b * 128:(b + 1) * 128]
        for d in range(3):
            kbk = b + d - 1
            if kbk < 0 or kbk >= NB:
                continue
            pk = pos[kbk * 128:(kbk + 1) * 128]
            okk = ok[kbk * 128:(kbk + 1) * 128].copy()
            if kbk in lead_blocks:
                okk[:] = False
            allowed = okk[:, None] & (np.abs(pq[None, :] - pk[:, None]) <= 128)
            flag = allowed.any(axis=1)
            mine = tri[d] & flag[:, None]
            qreal = ok[b * 128:(b + 1) * 128]
            assert np.array_equal(mine[:, qreal], allowed[:, qreal]), ("mask mismatch", b, d)
            kbt[flag, d * NB + b] = 0.0
    return cosT, sinT, kbt


_PREP_CACHE = {}


def kernel(x_prompt, x_sample, meta_tokens, norm_w, attn_w_in, attn_w_out, attn_sink, conv_w_in, conv_w, conv_w_out, final_norm_w):
    f32 = np.float32
    x_prompt = np.asarray(x_prompt, f32)
    x_sample = np.asarray(x_sample, f32)
    meta_tokens = np.asarray(meta_tokens, f32)
    lead = np.concatenate([np.zeros((112, D), f32), meta_tokens], 0)
    zeros_blk = np.zeros((128, D), f32)

    def chunked(w):
        n = w.shape[1] // 128
        return np.ascontiguousarray(w.reshape(KC, 128, n, 128).transpose(2, 1, 0, 3)).reshape(n, 128, D)

    attn_w_in = np.asarray(attn_w_in, f32)
    attn_w_out = np.asarray(attn_w_out, f32)
    conv_w_in = np.asarray(conv_w_in, f32)
    conv_w_out = np.asarray(conv_w_out, f32)
    w_ai = np.empty((2, 4, 10, 128, D), f32)
    for j in range(2):
        ch = chunked(attn_w_in[j])
        for g in range(4):
            idx = [4 * g + h for h in range(4)] + [16 + g, 20 + g] + [24 + 4 * g + h for h in range(4)]
            w_ai[j, g] = ch[idx]
    w_ai = w_ai.reshape(80, 128, D)
    w_ao = np.stack([chunked(attn_w_out[j]) for j in range(2)]).reshape(32, 128, D)
    w_ci = np.empty((2, KC, 4, 128, D), f32)
    for j in range(2):
        ch = chunked(conv_w_in[j])
        for f in range(KC):
            w_ci[j, f] = ch[[f, 16 + f, 32 + f, 48 + f]]
    w_ci = w_ci.reshape(128, 128, D)
    w_co = np.stack([chunked(conv_w_out[j]) for j in range(2)]).reshape(32, 128, D)

    norm_w = np.asarray(norm_w, f32)
    gvd = np.ascontiguousarray(norm_w.reshape(4, KC, 128).transpose(2, 0, 1)).reshape(128, 64)
    gfd = np.ascontiguousarray(np.asarray(final_norm_w, f32).reshape(KC, 128).T)
    skd = np.asarray(attn_sink, f32).reshape(32)
    cwd = np.ascontiguousarray(np.asarray(conv_w, f32).reshape(2, 3, KC, 128).transpose(3, 0, 1, 2)).reshape(128, 96)
    idd = np.eye(128, dtype=f32)
    ii = np.arange(128)
    tri_prev = np.where(ii[None, :] <= ii[:, None], 0.0, NEGB).astype(f32)
    tri_next = np.where(ii[:, None] <= ii[None, :], 0.0, NEGB).astype(f32)
    trid = np.concatenate([tri_prev, tri_next], 1)

    in_maps = []
    for c in range(NCORES):
        metas = _segment_meta(c)
        xs = np.empty((2, NB, 128, D), f32)
        xs[0, 0] = lead
        xs[0, 1:] = x_sample[c].reshape(16, 128, D)
        for vb, r in enumerate(metas[1][2]):
            if r == 0:
                xs[1, vb] = lead
            elif 1 <= r <= 64:
                xs[1, vb] = x_prompt[0, (r - 1) * 128:r * 128]
            else:
                xs[1, vb] = zeros_blk
        xs = np.ascontiguousarray(xs.reshape(2, NB, 128, KC, 128).transpose(0, 1, 4, 3, 2)).reshape(2, NB, 128, D)
        cosd = np.empty((2, 128, L), f32)
        sind = np.empty((2, 128, L), f32)
        okd = np.empty((2, L), f32)
        kbd = np.empty((2, 128, 3 * NB), f32)
        for sgi, (pos, ok, _real, _k) in enumerate(metas):
            key = (sgi, c if sgi == 1 else 0)
            if key not in _PREP_CACHE:
                _PREP_CACHE[key] = _tables(pos, ok, {vb for vb, r in enumerate(_real) if r == 0})
            cosd[sgi], sind[sgi], kbd[sgi] = _PREP_CACHE[key]
            okd[sgi] = ok.astype(f32)
        in_maps.append({"xs": xs, "cosd": cosd, "sind": sind, "okd": okd, "kbd": kbd, "w_ai": w_ai, "w_ao": w_ao,
                        "w_ci": w_ci, "w_co": w_co, "gvd": gvd, "gfd": gfd, "skd": skd, "cwd": cwd, "idd": idd, "trid": trid})

    nc = build_nc()
    res = run_bass_kernel_spmd(nc, in_maps, core_ids=list(range(NCORES)))
    def tok_major(a):
        a = np.asarray(a, f32)
        n = a.shape[0]
        return np.ascontiguousarray(a.reshape(n, 128, KC, 128).transpose(0, 3, 2, 1)).reshape(n * 128, D)

    y_sample = np.stack([tok_major(res.results[c]["ys"]) for c in range(NCORES)], 0)
    y_prompt = np.concatenate([tok_major(res.results[c]["yp"]) for c in range(NCORES)], 0).reshape(1, 8192, D)
    return (y_prompt, y_sample)
```
